# Optimizing a Trainium2 kernel written in Bass

```python
import jax, jax.numpy as jnp
from jax import lax
import numpy as np

D_MODEL = 1024
BATCH = 16
SEQ = 256
DEPTH = 1
DEC_BATCH = 8
DEC_SEQ = 2048
PAST_LEN = 512

GRID_W = 64
N_HEADS = 16
N_KV_HEADS = 4
HEAD_DIM = 64
ATT_WIDTH = N_HEADS * HEAD_DIM
KV_WIDTH = N_KV_HEADS * HEAD_DIM
CONV_WIDTH = D_MODEL
CONV_K = 3
D_FF = 2816
Q_BLOCK = 128
ROPE_THETA = 10000.0
AXIS_DIM = HEAD_DIM // 2
N_ADA = 6
EPS = 1e-6
IN_SPLITS = (ATT_WIDTH, KV_WIDTH, KV_WIDTH, CONV_WIDTH, CONV_WIDTH, CONV_WIDTH, D_MODEL, D_MODEL)
IN_WIDTH = 3 * CONV_WIDTH + ATT_WIDTH + 2 * KV_WIDTH + 2 * D_MODEL

kernel_name = "hybrid_diffusion_prefix_gqa_shortconv_step"


def rms_norm(x, g):
    xf = x.astype(jnp.float32)
    y = xf * lax.rsqrt(jnp.mean(xf * xf, axis=-1, keepdims=True) + EPS)
    return (y * g.astype(jnp.float32)).astype(x.dtype)


def dwconv3(x, w):
    xp = jnp.pad(x, ((0, 0), (1, 1), (0, 0)))
    return w[0] * xp[:, :-2] + w[1] * xp[:, 1:-1] + w[2] * xp[:, 2:]


def axial_rope_tables(n):
    rows = n // GRID_W
    row = jnp.repeat(jnp.arange(rows, dtype=jnp.float32), GRID_W)
    col = jnp.tile(jnp.arange(GRID_W, dtype=jnp.float32), rows)
    inv = jnp.power(ROPE_THETA, -jnp.arange(0, AXIS_DIM, 2, dtype=jnp.float32) / AXIS_DIM)
    ang_r = row[:, None] * inv[None, :]
    ang_c = col[:, None] * inv[None, :]
    return (jnp.cos(ang_r), jnp.sin(ang_r), jnp.cos(ang_c), jnp.sin(ang_c))


def _rotate(xh, cos, sin):
    half = xh.shape[-1] // 2
    x1, x2 = xh[..., :half], xh[..., half:]
    c = cos[None, :, None, :]
    s = sin[None, :, None, :]
    return jnp.concatenate([x1 * c - x2 * s, x1 * s + x2 * c], axis=-1)


def apply_axial_rope(x, tabs):
    cr, sr, cc, sc = tabs
    xf = x.astype(jnp.float32)
    out = jnp.concatenate([_rotate(xf[..., :AXIS_DIM], cr, sr), _rotate(xf[..., AXIS_DIM:], cc, sc)], axis=-1)
    return out.astype(x.dtype)


def block_attention(q, k, v):
    B, N, H, D = q.shape
    KV = k.shape[2]
    G = H // KV
    qb_len = Q_BLOCK if N % Q_BLOCK == 0 else N
    nb = N // qb_len
    scale = D ** -0.5
    qb = q.reshape(B, nb, qb_len, KV, G, D).transpose(1, 0, 2, 3, 4, 5)

    def one_block(qblk):
        s = jnp.einsum('bqkgd,btkd->bkgqt', qblk, k).astype(jnp.float32) * scale
        p = jax.nn.softmax(s, axis=-1).astype(v.dtype)
        return jnp.einsum('bkgqt,btkd->bqkgd', p, v)

    o = lax.map(one_block, qb)
    return o.transpose(1, 0, 2, 3, 4, 5).reshape(B, N, H * D)


def mixer(u, p, rope_tabs, ctx_k, ctx_v):
    B, N, _ = u.shape
    z = u @ p["w_in"]
    idx = np.cumsum(np.array(IN_SPLITS))[:-1].tolist()
    q, k, v, b_gate, c_gate, x_in, g_att, g_conv = jnp.split(z, idx, axis=-1)
    q = rms_norm(q.reshape(B, N, N_HEADS, HEAD_DIM), p["q_norm"])
    k = rms_norm(k.reshape(B, N, N_KV_HEADS, HEAD_DIM), p["k_norm"])
    v = v.reshape(B, N, N_KV_HEADS, HEAD_DIM)
    if rope_tabs is None:
        q_used, keys, vals = q, k, v
    else:
        q_used = apply_axial_rope(q, rope_tabs)
        keys = jnp.concatenate([ctx_k.astype(k.dtype), apply_axial_rope(k, rope_tabs)], axis=1)
        vals = jnp.concatenate([ctx_v.astype(v.dtype), v], axis=1)
    att = block_attention(q_used, keys, vals) @ p["w_att_out"]
    conv = (b_gate * dwconv3(c_gate * x_in, p["conv_w"])) @ p["w_conv_out"]
    merged = jax.nn.sigmoid(g_att) * att + jax.nn.sigmoid(g_conv) * conv
    return merged @ p["w_o"], k, v


def conv_ffn(u, p):
    up = dwconv3(u @ p["w_up"], p["conv_ffn"])
    g, val = jnp.split(up, 2, axis=-1)
    return (jax.nn.silu(g) * val) @ p["w_down"]


def trunk_layer(h, ada, p, rope_tabs, ctx_k, ctx_v):
    sh1, sc1, g1, sh2, sc2, g2 = jnp.split(ada, N_ADA, axis=-1)
    u = rms_norm(h, p["g_pre1"]) * (1 + sc1) + sh1
    mo, k, v = mixer(u, p, rope_tabs, ctx_k, ctx_v)
    h = h + g1 * rms_norm(mo, p["g_post1"])
    u = rms_norm(h, p["g_pre2"]) * (1 + sc2) + sh2
    h = h + g2 * rms_norm(conv_ffn(u, p), p["g_post2"])
    return h, k, v


def setup_inputs(seed: int = 0) -> dict:
    key = jax.random.key(seed)
    ks = jax.random.split(key, 24)
    f32 = jnp.float32
    nrm = lambda k, shape, s: (jax.random.normal(k, shape, f32) * s)
    gain = lambda k, shape: 1.0 + 0.05 * jax.random.normal(k, shape, f32)
    return {
        "x_prompt": nrm(ks[0], (BATCH, SEQ, D_MODEL), 1.0),
        "x_sample": nrm(ks[1], (DEC_BATCH, DEC_SEQ, D_MODEL), 1.0),
        "cache_k": nrm(ks[2], (DEC_BATCH, DEPTH, PAST_LEN, N_KV_HEADS, HEAD_DIM), 1.0),
        "cache_v": nrm(ks[3], (DEC_BATCH, DEPTH, PAST_LEN, N_KV_HEADS, HEAD_DIM), 1.0),
        "c": nrm(ks[4], (DEC_BATCH, D_MODEL), 1.0),
        "c_ctx": nrm(ks[5], (D_MODEL,), 1.0),
        "w_ada": nrm(ks[6], (DEPTH, D_MODEL, N_ADA * D_MODEL), 0.5 * D_MODEL ** -0.5),
        "b_ada": nrm(ks[7], (DEPTH, N_ADA * D_MODEL), 0.01),
        "g_pre1": gain(ks[8], (DEPTH, D_MODEL)),
        "g_post1": gain(ks[9], (DEPTH, D_MODEL)),
        "g_pre2": gain(ks[10], (DEPTH, D_MODEL)),
        "g_post2": gain(ks[11], (DEPTH, D_MODEL)),
        "w_in": nrm(ks[12], (DEPTH, D_MODEL, IN_WIDTH), D_MODEL ** -0.5),
        "q_norm": gain(ks[13], (DEPTH, HEAD_DIM)),
        "k_norm": gain(ks[14], (DEPTH, HEAD_DIM)),
        "w_att_out": nrm(ks[15], (DEPTH, ATT_WIDTH, D_MODEL), ATT_WIDTH ** -0.5),
        "conv_w": nrm(ks[16], (DEPTH, CONV_K, CONV_WIDTH), CONV_K ** -0.5),
        "w_conv_out": nrm(ks[17], (DEPTH, CONV_WIDTH, D_MODEL), CONV_WIDTH ** -0.5),
        "w_o": nrm(ks[18], (DEPTH, D_MODEL, D_MODEL), D_MODEL ** -0.5),
        "w_up": nrm(ks[19], (DEPTH, D_MODEL, 2 * D_FF), D_MODEL ** -0.5),
        "conv_ffn": nrm(ks[20], (DEPTH, CONV_K, 2 * D_FF), CONV_K ** -0.5),
        "w_down": nrm(ks[21], (DEPTH, D_FF, D_MODEL), D_FF ** -0.5),
    }


def reference(x_prompt, x_sample, cache_k, cache_v, c, c_ctx, w_ada, b_ada, g_pre1, g_post1, g_pre2, g_post2,
              w_in, q_norm, k_norm, w_att_out, conv_w, w_conv_out, w_o, w_up, conv_ffn, w_down):
    rope_tabs = axial_rope_tables(x_sample.shape[1])
    h_p = x_prompt
    h_s = x_sample
    new_ks = []
    new_vs = []
    for i in range(DEPTH):
        p = {
            "g_pre1": g_pre1[i], "g_post1": g_post1[i], "g_pre2": g_pre2[i], "g_post2": g_post2[i],
            "w_in": w_in[i], "q_norm": q_norm[i], "k_norm": k_norm[i], "w_att_out": w_att_out[i],
            "conv_w": conv_w[i], "w_conv_out": w_conv_out[i], "w_o": w_o[i],
            "w_up": w_up[i], "conv_ffn": conv_ffn[i], "w_down": w_down[i],
        }
        ada_ctx = (jax.nn.silu(c_ctx) @ w_ada[i] + b_ada[i])[None, None, :]
        ada_lat = (jax.nn.silu(c) @ w_ada[i] + b_ada[i])[:, None, :]
        h_p, k_ctx, v_ctx = trunk_layer(h_p, ada_ctx, p, None, None, None)
        new_ks.append(k_ctx)
        new_vs.append(v_ctx)
        h_s, _, _ = trunk_layer(h_s, ada_lat, p, rope_tabs, cache_k[:, i], cache_v[:, i])
    new_k = jnp.stack(new_ks, axis=1)
    new_v = jnp.stack(new_vs, axis=1)
    return (h_p, h_s, new_k, new_v)
```

```python
import numpy as np
from contextlib import ExitStack

import concourse.bass as bass
import concourse.mybir as mybir
from concourse.ap import AP
from concourse.bass_utils import run_bass_kernel_spmd

F32 = mybir.dt.float32
BF16 = mybir.dt.bfloat16
AF = mybir.ActivationFunctionType
ALU = mybir.AluOpType

D = 1024
KC = 8
TT = 512
NS = 2048
NP = 512
PAST = 512
NKEY = PAST + NS
DFF = 2816
NFC = 22
IN_W = 6656
EPS = 1e-6
C_Q, C_K, C_V, C_B, C_C, C_X, C_GA, C_GC = 0, 1024, 1280, 1536, 2560, 3584, 4608, 5632
N_WSLOT = 6
GRID_W = 64


class _Op:
    __slots__ = ("eng", "emit", "waits", "tok", "dma_sem", "signal", "stage")

    def __init__(self, eng, emit, waits, tok, dma_sem, stage):
        self.eng, self.emit, self.waits, self.tok, self.dma_sem = eng, emit, waits, tok, dma_sem
        self.signal = False
        self.stage = stage


class Prog:
    ENGS = ("pe", "act", "dve", "pool", "sp")

    def __init__(self):
        self.streams = {e: [] for e in self.ENGS}
        self.reg = {}
        self.dma_cnt = {}
        self.batch = set()
        self.store_sems = set()
        self.stage = "init"

    def op(self, eng, emit, acc, dma_sem=None, is_store=False):
        stream = self.streams[eng]
        if dma_sem is not None:
            cnt = self.dma_cnt.get(dma_sem, 0) + 16
            self.dma_cnt[dma_sem] = cnt
            tok = ("d", dma_sem, cnt)
            if is_store:
                self.store_sems.add(dma_sem)
        else:
            tok = ("e", eng, len(stream))
        rkey = tok[1] if tok[0] == "e" else tok
        deps = {}
        for reg, mode in acc:
            st = self.reg.get(reg)
            if st is None:
                st = [None, {}]
                self.reg[reg] = st
            lw, readers = st
            if mode == "r":
                if lw is not None:
                    deps[lw] = True
            elif mode == "x":
                if lw is not None:
                    deps[lw] = True
                for r in readers.values():
                    deps.setdefault(r, False)
            else:
                if lw is not None:
                    deps.setdefault(lw, False)
                for r in readers.values():
                    deps.setdefault(r, False)
        for reg, mode in acc:
            st = self.reg[reg]
            if mode == "r":
                st[1][rkey] = tok
            elif mode == "x":
                st[1] = {rkey: tok}
        for reg, mode in acc:
            if mode == "w":
                st = self.reg[reg]
                st[0] = tok
                st[1] = {}
        waits = []
        for t, raw in deps.items():
            if t == tok:
                continue
            if t[0] == "d" and dma_sem is not None and t[1] == dma_sem:
                continue
            if t[0] == "e" and t[1] == eng:
                if eng in ("pe", "sp"):
                    continue
            if t[0] == "e":
                self.streams[t[1]][t[2]].signal = True
            waits.append(t)
        stream.append(_Op(eng, emit, waits, tok, dma_sem, self.stage))

    def emit(self, nc, es):
        for e in self.ENGS:
            for o in self.streams[e]:
                o.signal = False
        for eng in self.ENGS:
            waited = {}
            for o in self.streams[eng]:
                pend = {}
                for t in o.waits:
                    if t[0] == "e":
                        key, v = ("e", t[1]), t[2]
                    else:
                        key = ("d", t[1])
                        v = self.dma_cnt[t[1]] if t[1] in self.batch else t[2]
                    if waited.get(key, -1) >= v:
                        continue
                    waited[key] = v
                    pend[key] = t
                o.waits = list(pend.values())
                for t in o.waits:
                    if t[0] == "e":
                        self.streams[t[1]][t[2]].signal = True
        sig = {}
        for e in self.ENGS:
            c = 0
            arr = []
            for o in self.streams[e]:
                if o.dma_sem is None and o.signal:
                    c += 1
                arr.append(c)
            sig[e] = arr
        esem = {e: es.enter_context(nc.semaphore("sem_" + e)) for e in ("pe", "act", "dve", "pool")}
        dsem = {}
        for i, k in enumerate(self.dma_cnt):
            dsem[k] = es.enter_context(nc.semaphore("dsem%d" % i))
        block = es.enter_context(nc.Block())
        hooks = {"pe": block.tensor, "act": block.scalar, "dve": block.vector, "pool": block.gpsimd, "sp": block.sync}
        for eng in self.ENGS:
            ops = self.streams[eng]

            def body(e, ops=ops, eng=eng):
                for o in ops:
                    pend = []
                    for t in o.waits:
                        if t[0] == "e":
                            pend.append((esem[t[1]], sig[t[1]][t[2]]))
                        else:
                            pend.append((dsem[t[1]], self.dma_cnt[t[1]] if t[1] in self.batch else t[2]))
                    attach = pend.pop() if (pend and o.dma_sem is None and eng in ("pe", "act", "dve")) else None
                    for sem, val in pend:
                        e.wait_ge(sem, val)
                    ins = o.emit(e)
                    if attach is not None:
                        ins._wait_ge(attach[0], attach[1])
                    if o.dma_sem is not None:
                        ins.then_inc(dsem[o.dma_sem], 16)
                    elif o.signal:
                        ins.then_inc(esem[eng], 1)
                if eng == "sp":
                    for k in self.store_sems:
                        e.wait_ge(dsem[k], self.dma_cnt[k])

            hooks[eng](body)


def build_program():
    nc = bass.Bass("TRN2", target_bir_lowering=False)

    def din(name, shape):
        return nc.dram_tensor(name, list(shape), F32, kind="ExternalInput").ap()

    def dout(name, shape):
        return nc.dram_tensor(name, list(shape), F32, kind="ExternalOutput").ap()

    x_s = din("x_s", [NS, D])
    x_p = din("x_p", [NP, D])
    ck = din("ck", [PAST, 256])
    cv = din("cv", [PAST, 256])
    c2_d = din("c2", [128, KC * 2])
    bada_d = din("bada", [128, 48])
    gvec_d = din("gvec", [128, 4 * KC])
    qkg_d = din("qkg", [128, 2])
    convw_d = din("convw", [128, KC * 3])
    convf_d = din("convf", [128, 44 * 3])
    ident_d = din("ident", [128, 128])
    pmat_d = din("pmat", [128, 128])
    cos_d = din("cos", [128, NS])
    sin_d = din("sin", [128, NS])
    w_ada = din("w_ada", [D, 6 * D])
    w_in = din("w_in", [D, IN_W])
    w_att = din("w_att", [D, D])
    w_cvo = din("w_cvo", [D, D])
    w_o = din("w_o", [D, D])
    w_up = din("w_up", [D, 2 * DFF])
    w_dn = din("w_dn", [DFF, D])
    y_s = dout("y_s", [NS, D])
    y_p = dout("y_p", [NP, D])
    nk_o = dout("nk", [NP, 256])
    nv_o = dout("nv", [NP, 256])

    es = ExitStack()
    with es:
        def sb(name, shape, dt):
            return es.enter_context(nc.sbuf_tensor("sb_" + name, list(shape), dt))

        hT = sb("hT", [128, KC, NS], F32)
        KT = sb("KT", [128, 4 * NKEY], BF16)
        V = sb("V", [128, 20, 4, 128], BF16)
        UB = sb("UB", [128, KC, 10], BF16)
        U2B = sb("U2B", [128, KC, 10], BF16)
        U = sb("U", [128, KC, TT], BF16)
        SQ = [sb("SQ%d" % i, [128, TT], BF16) for i in range(2)]
        RS = sb("RS", [128, TT], F32)
        T = [sb("T%d" % i, [128, TT], F32) for i in range(4)]
        QT = [[sb("QT%d_%d" % (i, h), [128, TT], BF16) for h in range(2)] for i in range(2)]
        PT = [sb("PT%d" % i, [128, 2, TT], BF16) for i in range(3)]
        G24 = sb("G24", [128, 24, TT], BF16)
        MO = sb("MO", [128, KC, TT], F32)
        XBV = [sb("XBV%d" % i, [128, 2, 516], F32) for i in range(3)]
        XB = XBV[0][:, 0, :]
        WS = [sb("WS%d" % i, [128, KC, 128], BF16) for i in range(N_WSLOT)]
        COS = sb("COS", [128, TT], F32)
        SIN = sb("SIN", [128, TT], F32)
        ident = sb("ident", [128, 128], F32)
        pmat = sb("pmat", [128, 128], F32)
        ones_bf = sb("ones_bf", [128, 128], BF16)
        oblk_bf = sb("oblk_bf", [128, 128], BF16)
        c2 = sb("c2", [128, KC, 2], F32)
        sc2 = sb("sc2", [128, KC, 2], F32)
        sc2b = sb("sc2b", [128, KC, 2], BF16)
        SQe = sb("SQe", [128, KC, 2], BF16)
        RSe = sb("RSe", [128, 2], F32)
        Te = sb("Te", [128, KC, 2], F32)
        bada = sb("bada", [128, 48], F32)
        adaT = sb("adaT", [128, 48, 2], F32)
        gvec = sb("gvec", [128, 4, KC], F32)
        MOD = sb("MOD", [128, 6, KC, 2], F32)
        qkg = sb("qkg", [128, 2], F32)
        convw = sb("convw", [128, KC, 3], F32)
        convf = sb("convf", [128, 44, 3], F32)
        PS = [es.enter_context(nc.psum_tensor("PS%d" % i, [128, 2, TT], F32)) for i in range(4)]

        KTv = KT[:].rearrange("p (g n) -> p g n", g=4)
        Vf = V[:].rearrange("p b g e -> p (b g e)").bitcast(F32)
        FT = [Vf[:, j * TT:(j + 1) * TT] for j in range(10)]
        FTR = [[("V", 2 * j), ("V", 2 * j + 1)] for j in range(10)]
        MOs = MO[:].rearrange("p k t -> p (k t)")
        ATT = lambda c: G24[:, c, :]
        CONV = lambda c: G24[:, 8 + c, :]
        MERGED = lambda c: G24[:, 16 + c, :]
        ACTF = lambda f: G24[:, f, :]

        P = Prog()
        P.batch.update(["const", "cache", "cachev"])

        def bank(b):
            return PS[b // 2][:, b % 2, :]

        def breg(b):
            return ("ps", b)

        bank_rr = [0]
        reserved = set()

        def nextbank():
            while True:
                b = bank_rr[0]
                bank_rr[0] = (b + 1) % 8
                if b not in reserved:
                    return b

        def mm(out, lhsT, rhs, start, stop, acc):
            P.op("pe", lambda e: e.matmul(out, lhsT=lhsT, rhs=rhs, start=start, stop=stop), acc)

        def tr(out, in_, idn, acc):
            P.op("pe", lambda e: e.transpose(out=out, in_=in_, identity=idn), acc)

        def act(out, in_, func, acc, bias=None, scale=None):
            kw = {}
            if bias is not None:
                kw["bias"] = bias
            if scale is not None:
                kw["scale"] = scale
            P.op("act", lambda e: e.activation(out=out, in_=in_, func=func, **kw), acc)

        def tt(out, in0, in1, op, acc, eng="dve"):
            P.op(eng, lambda e: e.tensor_tensor(out=out, in0=in0, in1=in1, op=op), acc)

        def ts(out, in0, s1, s2, op0, op1, acc, eng="dve"):
            if s2 is None:
                P.op(eng, lambda e: e.tensor_scalar(out=out, in0=in0, scalar1=s1, scalar2=None, op0=op0), acc)
            else:
                P.op(eng, lambda e: e.tensor_scalar(out=out, in0=in0, scalar1=s1, scalar2=s2, op0=op0, op1=op1), acc)

        def stt(out, in0, scalar, in1, op0, op1, acc, eng="dve"):
            P.op(eng, lambda e: e.scalar_tensor_tensor(out=out, in0=in0, scalar=scalar, in1=in1, op0=op0, op1=op1), acc)

        def cp(out, in_, acc, eng="dve"):
            P.op(eng, lambda e: e.tensor_copy(out=out, in_=in_), acc)

        def recip(out, in_, acc):
            P.op("dve", lambda e: e.reciprocal(out=out, in_=in_), acc)

        def memset(ap, val, acc, eng="dve"):
            P.op(eng, lambda e: e.memset(ap, val), acc)

        def dma(q, out, in_, acc, sem, is_store=False):
            P.op(q, lambda e: e.dma_start(out=out, in_=in_), acc, dma_sem=sem, is_store=is_store)

        wslot_rr = [0]

        def load_w(srcs):
            s = wslot_rr[0]
            wslot_rr[0] = (s + 1) % N_WSLOT
            for c0, src in srcs:
                kcn, ncol = src.shape[1], src.shape[2]
                if ncol > 128:
                    dst = WS[s][:].rearrange("p k c -> p (k c)")[:, 0:kcn * ncol].rearrange("p (k c) -> p k c", c=ncol)
                else:
                    dst = WS[s][:, 0:kcn, c0:c0 + ncol]
                dma("pool", dst, src, [(("ws", s), "w")], ("w", s))
            return WS[s], ("ws", s)

        def wcols(w, c0, ncol=128):
            return w.rearrange("(k p) c -> p k c", p=128)[:, :, c0:c0 + ncol]

        def load_w1(w, c0):
            return load_w([(0, wcols(w, c0))])

        WD = [KT[:, j * 2816:(j + 1) * 2816].rearrange("p (k c) -> p k c", c=128) for j in range(3)]
        WD_KT = [(0, 1), (1, 2), (2, 3)]
        wd_rr = [0]

        def load_wd(oc):
            j = wd_rr[0]
            wd_rr[0] = (j + 1) % 3
            regs = [("wd", j)] + [("KT", g) for g in WD_KT[j]]
            src = w_dn.rearrange("(k p) c -> p k c", p=128)[:, :, oc * 128:(oc + 1) * 128]
            dma("pool", WD[j], src, [(r, "w") for r in regs], ("wd", j))
            return WD[j], regs

        def cload(dst, src, reg):
            dma("sp", dst, src, [(reg, "w")], "const")

        cload(ident[:], ident_d, "ident")
        cload(pmat[:], pmat_d, "pmat")
        cload(c2[:], c2_d.rearrange("p (k g) -> p k g", g=2), "c2")
        cload(bada[:], bada_d, "bada")
        cload(gvec[:], gvec_d.rearrange("p (a k) -> p a k", k=KC), "gvec")
        cload(qkg[:], qkg_d, "qkg")
        cload(convw[:], convw_d.rearrange("p (k j) -> p k j", j=3), "convw")
        cload(convf[:], convf_d.rearrange("p (k j) -> p k j", j=3), "convf")
        memset(ones_bf[:], 1.0, [("ones", "w")])
        memset(oblk_bf[:], 0.0, [("oblk", "w")])
        memset(oblk_bf[0:64, 0:64], 1.0, [("oblk", "w")])
        memset(oblk_bf[64:128, 64:128], 1.0, [("oblk", "w")])
        memset(V[:, :, :, 64:128], 1.0, [(("V", b_), "w") for b_ in range(20)])
        for i_ in range(2):
            memset(QT[i_][0][64:128, :], 0.0, [(("QT", i_), "w")])
            memset(QT[i_][1][0:64, :], 0.0, [(("QT", i_), "w")])
        memset(UB[:], 0.0, [("UB", "w")])
        memset(U2B[:], 0.0, [("U2B", "w")])
        for i_ in range(3):
            memset(XBV[i_][:], 0.0, [(("XBV", i_), "w")])

        def hslice(slot, k, c0=0, n=TT):
            return hT[:, k, slot * TT + c0: slot * TT + c0 + n]

        MO_ALL_W = [(("MO", k), "w") for k in range(KC)]
        MO_ALL_R = [(("MO", k), "r") for k in range(KC)]

        def load_xT(x_rows, slot):
            stg = MOs.rearrange("p (b d) -> p b d", d=D)
            dma("sp", stg, x_rows.rearrange("(b p) d -> p b d", p=128), MO_ALL_W, ("xs", 0))
            for k in range(KC):
                b = nextbank()
                for blk in range(4):
                    tr(bank(b)[:, blk * 128:(blk + 1) * 128], stg[:, blk, k * 128:(k + 1) * 128], ident[:],
                       MO_ALL_R + [("ident", "r"), (breg(b), "w")])
                act(hslice(slot, k), bank(b), AF.Copy, [(breg(b), "x"), (("hT", slot, k), "w")])

        P.stage = "load_xT"
        load_xT(x_p, 0)
        for i_ in range(1, 4):
            load_xT(x_s[i_ * TT:(i_ + 1) * TT, :], i_)
        P.stage = "ada"
        act(sc2[:], c2[:], AF.Sigmoid, [("c2", "r"), ("sc2", "w")])
        tt(sc2[:], sc2[:], c2[:], ALU.mult, [("c2", "r"), ("sc2", "r"), ("sc2", "w")])
        cp(sc2b[:], sc2[:], [("sc2", "r"), ("sc2b", "w")])
        MOb = MOs.bitcast(BF16)
        G24s = G24[:].rearrange("p c t -> p (c t)")
        for j in range(12):
            slot = j % 3
            m_ = slot
            stg = G24s[:, m_ * 4096:(m_ + 1) * 4096].rearrange("p (k c) -> p k c", c=512)
            sregs = [("G", 8 * m_ + q_) for q_ in range(8)]
            dma("pool", stg, wcols(w_ada, j * 512, 512), [(r, "w") for r in sregs], ("ada", slot))
            b = nextbank()
            for q_ in range(4):
                for k in range(KC):
                    mm(bank(b)[:, 2 * q_:2 * q_ + 2], stg[:, k, q_ * 128:(q_ + 1) * 128], sc2b[:, k, :], k == 0, k == KC - 1,
                       [(r, "r") for r in sregs] + [("sc2b", "r"), (breg(b), "w")])
            a_ = bada[:, 4 * j:4 * j + 1]
            bb_ = AP(a_.tensor, a_.offset, [list(a_.ap[0]), [1, 4], [0, 2]])
            tt(adaT[:, 4 * j:4 * j + 4, :], bank(b)[:, 0:8].rearrange("p (q e) -> p q e", e=2), bb_, ALU.add,
               [(breg(b), "x"), ("bada", "r"), ("adaT", "w")])
        for half in range(2):
            o = 24 * half
            gpre = gvec[:, 2 * half, :].unsqueeze(2).to_broadcast([128, KC, 2])
            gpost = gvec[:, 2 * half + 1, :].unsqueeze(2).to_broadcast([128, KC, 2])
            a_reg = [("adaT", "r"), ("gvec", "r"), ("MOD", "r"), ("MOD", "w")]
            ts(MOD[:, 3 * half + 0], adaT[:, o + 8:o + 16, :], 1.0, None, ALU.add, None, a_reg)
            tt(MOD[:, 3 * half + 0], MOD[:, 3 * half + 0], gpre, ALU.mult, a_reg)
            cp(MOD[:, 3 * half + 1], adaT[:, o:o + 8, :], a_reg)
            tt(MOD[:, 3 * half + 2], adaT[:, o + 16:o + 24, :], gpost, ALU.mult, a_reg)

        def modcol(which, k, grp):
            return MOD[:, which, k, grp:grp + 1]

        def rstd_gen(src_of_k, regs_of_k, nchunks, lhs_ones, ones_reg, inv_n, out_rs, out_reg, sbank, dve_sq=False):
            for k in range(nchunks):
                s_ = k % 2
                if dve_sq:
                    tt(SQ[s_][:], src_of_k(k), src_of_k(k), ALU.mult, regs_of_k(k) + [(("SQ", s_), "w")])
                else:
                    act(SQ[s_][:], src_of_k(k), AF.Square, regs_of_k(k) + [(("SQ", s_), "w")])
                yield
                mm(bank(sbank), lhs_ones, SQ[s_][:], k == 0, k == nchunks - 1,
                   [(("SQ", s_), "r"), (ones_reg, "r"), (breg(sbank), "w")])
            yield
            act(out_rs[:], bank(sbank), AF.Ln, [(breg(sbank), "x"), (out_reg, "w")], bias=EPS, scale=inv_n)
            act(out_rs[:], out_rs[:], AF.Exp, [(out_reg, "r"), (out_reg, "w")], scale=-0.5)
            yield

        def run(g):
            for _ in g:
                pass

        def interleave(main, side, ratio=1):
            n = 0
            for _ in main:
                n += 1
                if n % ratio == 0:
                    next(side, None)
            for _ in side:
                pass

        def chain(*gens):
            for g in gens:
                yield from g

        def pre_norm_gen(slot, which, grp, dve_sq=False):
            sbank = nextbank()
            reserved.add(sbank)
            yield from rstd_gen(lambda k: hslice(slot, k), lambda k: [(("hT", slot, k), "r")], KC, ones_bf[:], "ones",
                                1.0 / D, RS, "RS", sbank, dve_sq)
            reserved.discard(sbank)
            for k in range(KC):
                t = T[k % 2]
                tt(t[:], hslice(slot, k), RS[:], ALU.mult, [(("hT", slot, k), "r"), ("RS", "r"), (("T", k % 2), "w")])
                yield
                act(U[:, k, :], t[:], AF.Identity, [(("T", k % 2), "r"), ("MOD", "r"), (("U", k), "w")],
                    bias=modcol(3 * which + 1, k, grp), scale=modcol(3 * which + 0, k, grp))

        def pre_norm(slot, which, grp):
            run(pre_norm_gen(slot, which, grp))

        def post_norm_gen(slot, which, grp):
            sbank = nextbank()
            reserved.add(sbank)
            yield from rstd_gen(lambda k: MO[:, k, :], lambda k: [(("MO", k), "r")], KC, ones_bf[:], "ones",
                                1.0 / D, RS, "RS", sbank)
            reserved.discard(sbank)
            for k in range(KC):
                t = T[k % 2]
                stt(t[:], MO[:, k, :], modcol(3 * which + 2, k, grp), RS[:], ALU.mult, ALU.mult,
                    [(("MO", k), "r"), ("MOD", "r"), ("RS", "r"), (("T", k % 2), "w")])
                tt(hslice(slot, k), hslice(slot, k), t[:], ALU.add,
                   [(("hT", slot, k), "r"), (("T", k % 2), "r"), (("hT", slot, k), "w")])
                yield

        def post_norm_residual(slot, which, grp):
            run(post_norm_gen(slot, which, grp))

        def rstd_from(*args):
            run(rstd_gen(*args))

        TS_A = dict(sq=SQ[0][:], sqr=("SQ", 0), rs=T[2][:], rsr=("T", 2), kn=T[3][:], knr=("T", 3),
                    ra=T[0][:], rar=("T", 0), rb=T[1][:], rbr=("T", 1))
        TS_B = dict(sq=SQ[1][:], sqr=("SQ", 1), rs=MO[:, 4, :], rsr=("MO", 4), kn=MO[:, 5, :], knr=("MO", 5),
                    ra=MO[:, 6, :], rar=("MO", 6), rb=MO[:, 7, :], rbr=("MO", 7))

        def headnorm_gen(b, gcol, rope, out_ap, out_acc, sbank, pbank, ts=None):
            ts = ts or TS_A
            act(ts["sq"], bank(b), AF.Square, [(breg(b), "x"), (ts["sqr"], "w")])
            yield
            mm(bank(sbank), oblk_bf[:], ts["sq"], True, True, [(ts["sqr"], "r"), ("oblk", "r"), (breg(sbank), "w")])
            act(ts["rs"], bank(sbank), AF.Ln, [(breg(sbank), "x"), (ts["rsr"], "w")], bias=EPS, scale=1.0 / 64)
            act(ts["rs"], ts["rs"], AF.Exp, [(ts["rsr"], "r"), (ts["rsr"], "w")], scale=-0.5)
            stt(ts["kn"], bank(b), gcol, ts["rs"], ALU.mult, ALU.mult,
                [(breg(b), "x"), ("qkg", "r"), (ts["rsr"], "r"), (ts["knr"], "w")])
            yield
            outs = out_ap if isinstance(out_ap, list) else [(slice(0, 128), out_ap)]
            if rope:
                mm(bank(pbank), pmat[:], ts["kn"], True, True, [("pmat", "r"), (ts["knr"], "r"), (breg(pbank), "w")])
                tt(ts["ra"], ts["kn"], COS[:], ALU.mult, [(ts["knr"], "r"), ("COS", "r"), (ts["rar"], "w")])
                tt(ts["rb"], bank(pbank), SIN[:], ALU.mult, [(breg(pbank), "x"), ("SIN", "r"), (ts["rbr"], "w")])
                yield
                for rows, o_ap in outs:
                    tt(o_ap, ts["ra"][rows, :], ts["rb"][rows, :], ALU.add, [(ts["rar"], "r"), (ts["rbr"], "r")] + out_acc)
            else:
                for rows, o_ap in outs:
                    cp(o_ap, ts["kn"][rows, :], [(ts["knr"], "r")] + out_acc)

        def headnorm(*args):
            for _ in headnorm_gen(*args):
                pass

        def v_lhsT(blk, g):
            return V[:, blk, g, :]

        def load_rope(i):
            dma("sp", COS[:], cos_d[:, i * TT:(i + 1) * TT], [("COS", "w")], ("rope", 0))
            dma("sp", SIN[:], sin_d[:, i * TT:(i + 1) * TT], [("SIN", "w")], ("rope", 1))

        def roundrobin(*gens):
            gens = list(gens)
            while gens:
                for g_ in list(gens):
                    if next(g_, "done") == "done":
                        gens.remove(g_)

        def kv_stage(kpos, vblk0, rope, prompt):
            bn = []
            if prompt:
                bn = [nextbank(), nextbank()]
                reserved.update(bn)

            def kgen(g):
                ts = TS_A if g % 2 == 0 else TS_B
                src = wcols(w_in, C_K + g * 64, 64)
                wt, wr = load_w([(0, src), (64, src)])
                b = nextbank()
                reserved.add(b)
                for k in range(KC):
                    mm(bank(b), wt[:, k, :], U[:, k, :], k == 0, k == KC - 1,
                       [(wr, "r"), (("U", k), "r"), (breg(b), "w")])
                sbank = nextbank()
                reserved.add(sbank)
                yield from headnorm_gen(b, qkg[:, 1:2], rope, KTv[:, g, kpos:kpos + TT], [(("KT", g), "w")],
                                        sbank, sbank, ts)
                reserved.discard(b)
                reserved.discard(sbank)
                if prompt:
                    yield
                    for blk in range(4):
                        bb = bn[blk // 2]
                        c0 = (blk % 2) * 256 + g * 64
                        tr(bank(bb)[:, c0:c0 + 64], ts["kn"][0:64, blk * 128:(blk + 1) * 128], ident[0:64, 0:64],
                           [(ts["knr"], "r"), ("ident", "r"), (breg(bb), "w")])

            def vgen():
                wv = [load_w1(w_in, C_V), load_w1(w_in, C_V + 128)]
                for blk in range(4):
                    b = nextbank()
                    for hv in range(2):
                        wt, wr = wv[hv]
                        for k in range(KC):
                            mm(bank(b)[:, hv * 128:(hv + 1) * 128], U[:, k, blk * 128:(blk + 1) * 128], wt[:, k, :],
                               k == 0, k == KC - 1, [(wr, "r"), (("U", k), "r"), (breg(b), "w")])
                    act(V[:, vblk0 + blk, :, 0:64], bank(b)[:, 0:256].rearrange("p (g e) -> p g e", e=64), AF.Copy,
                        [(breg(b), "x"), (("V", vblk0 + blk), "w")])
                    if prompt:
                        c0 = 1024 + blk * 256
                        cp(MOs[:, c0:c0 + 256], bank(b)[:, 0:256], [(breg(b), "x"), (("MO", 2 + blk // 2), "w")])
                    yield

            roundrobin(kgen(0), kgen(1))
            roundrobin(kgen(2), kgen(3), vgen())
            if prompt:
                for i2 in range(2):
                    act(MO[:, i2, :], bank(bn[i2]), AF.Copy, [(breg(bn[i2]), "x"), (("MO", i2), "w")])
                reserved.difference_update(bn)
                dma("sp", nk_o.rearrange("(b p) c -> p b c", p=128),
                    MOs[:, 0:1024].rearrange("p (b c) -> p b c", c=256),
                    [(("MO", 0), "r"), (("MO", 1), "r")], ("st", 1), is_store=True)
                dma("sp", nv_o.rearrange("(b p) c -> p b c", p=128),
                    MOs[:, 1024:2048].rearrange("p (b c) -> p b c", c=256),
                    [(("MO", 2), "r"), (("MO", 3), "r")], ("st", 2), is_store=True)

        def q_proj_gen(c, rope):
            wt, wr = load_w1(w_in, C_Q + c * 128)
            for k in range(KC):
                mm(bank(0), wt[:, k, :], U[:, k, :], k == 0, k == KC - 1,
                   [(wr, "r"), (("U", k), "r"), (breg(0), "w")])
                if k == 3:
                    yield
            qs = QT[c % 2]
            yield from headnorm_gen(0, qkg[:, 0:1], rope,
                                    [(slice(0, 64), qs[0][0:64, :]), (slice(64, 128), qs[1][64:128, :])],
                                    [(("QT", c % 2), "w")], 1, 1)

        def attention(prompt, conv_i=None, pre_side=None):
            for _ in q_proj_gen(0, not prompt):
                pass
            for c in range(8):
                qgen = q_proj_gen(c + 1, not prompt) if c + 1 < 8 else iter(())
                qsteps = (0, 0, 1, 2, 3) if prompt else (2, 3, 7, 11, 14)
                cgen = conv_chunk_gen(conv_i, c) if conv_i is not None else iter(())
                csteps = (8, 9, 12, 13, 14, 16, 17, 18)
                g = c // 2
                q = QT[c % 2]
                qreg = ("QT", c % 2)
                if prompt:
                    steps = [(hh, seq) for hh in range(2) for seq in range(2)]
                else:
                    steps = [(hh, j2) for hh in range(2) for j2 in range(10)]

                def s_step(n):
                    hh, j = steps[n]
                    rows = slice(hh * 64, hh * 64 + 64)
                    sd = 2 + (n % 2)
                    for half in range(2):
                        if prompt:
                            k0 = j * 256 + half * 128
                            mm(PS[sd][:, half, 0:256], KTv[:, g, k0:k0 + 128], q[hh][:, j * 256:(j + 1) * 256],
                               True, True, [(("KT", g), "r"), (qreg, "r"), (breg(2 * sd + half), "w")])
                        else:
                            k0 = (2 * j + half) * 128
                            mm(PS[sd][:, half, :], KTv[:, g, k0:k0 + 128], q[hh][:, :],
                               True, True, [(("KT", g), "r"), (qreg, "r"), (breg(2 * sd + half), "w")])

                def e_step(n):
                    sd = 2 + (n % 2)
                    pt = PT[n % 3]
                    acc = [(breg(2 * sd), "x"), (breg(2 * sd + 1), "x"), (("PT", n % 3), "w")]
                    if prompt:
                        act(pt[:, :, 0:256], PS[sd][:, :, 0:256], AF.Exp, acc, scale=0.125)
                    else:
                        act(pt[:], PS[sd][:], AF.Exp, acc, scale=0.125)

                def pv_step(n):
                    hh, j = steps[n]
                    ob = 2 + hh
                    pt = PT[n % 3]
                    for half in range(2):
                        if prompt:
                            blk = j * 2 + half
                            mm(bank(ob)[:, j * 256:(j + 1) * 256], v_lhsT(blk, g), pt[:, half, 0:256],
                               half == 0, half == 1,
                               [(("V", blk), "r"), (("PT", n % 3), "r"), (breg(ob), "w")])
                        else:
                            blk = 2 * j + half
                            mm(bank(ob), v_lhsT(blk, g), pt[:, half, :], blk == 0, blk == 19,
                               [(("V", blk), "r"), (("PT", n % 3), "r"), (breg(ob), "w")])

                def fin_gen(hh):
                    ob = 2 + hh
                    yield
                    act(RS[0:64, :], bank(ob)[64:128, :], AF.Ln, [(breg(ob), "x"), ("RS", "w")])
                    yield
                    act(RS[0:64, :], RS[0:64, :], AF.Exp, [("RS", "r"), ("RS", "w")], scale=-1.0)
                    tt(ATT(c)[hh * 64:(hh + 1) * 64, :], bank(ob)[0:64, :], RS[0:64, :], ALU.mult,
                       [(breg(ob), "x"), ("RS", "r"), (("G", c), "w")])

                fins = []

                ns = len(steps)

                def pv_and_fin(m):
                    pv_step(m)
                    if m + 1 == ns or steps[m + 1][0] != steps[m][0]:
                        fins.append(fin_gen(steps[m][0]))

                s_step(0)
                if ns > 1:
                    s_step(1)
                for n in range(ns):
                    e_step(n)
                    for fg in list(fins):
                        if next(fg, "done") == "done":
                            fins.remove(fg)
                    if n >= 1:
                        pv_and_fin(n - 1)
                    for _q in range(qsteps.count(n)):
                        next(qgen, None)
                    if conv_i is not None and n in csteps:
                        next(cgen, None)
                    if pre_side is not None and c == 0:
                        next(pre_side, None)
                    if n + 2 < ns:
                        s_step(n + 2)
                pv_and_fin(ns - 1)
                for fg in fins:
                    for _ in fg:
                        pass
                for _ in qgen:
                    pass
                for _ in cgen:
                    pass
                if pre_side is not None and c == 0:
                    for _ in pre_side:
                        pass

        def tile_views(prompt):
            if prompt:
                def v512(ap):
                    return ap.rearrange("p (s n) -> p s n", s=2)

                def xviews(buf):
                    b3 = buf[:, 0:516].rearrange("p (s n) -> p s n", s=2)
                    return b3[:, :, 1:257], b3[:, :, 0:256], b3[:, :, 2:258]
            else:
                def v512(ap):
                    return ap

                def xviews(buf):
                    return buf[:, 1:513], buf[:, 0:512], buf[:, 2:514]
            return v512, xviews

        def halo_cols(buf, k, i):
            a = buf[:, k, 2 * i:2 * i + 1]
            return AP(a.tensor, a.offset, [list(a.ap[0]), [3, 2]])

        def edge_cols(buf):
            a = buf[:, 0:1]
            return AP(a.tensor, a.offset, [list(a.ap[0]), [513, 2]])

        def conv_mixer(prompt, i):
            v512, xviews = tile_views(prompt)
            xc, xl, xr = xviews(XB)
            for c in range(8):
                wb, wbr = load_w1(w_in, C_B + c * 128)
                wc, wcr = load_w1(w_in, C_C + c * 128)
                wx, wxr = load_w1(w_in, C_X + c * 128)
                bb, bc, bx = nextbank(), nextbank(), nextbank()
                for (wt, wr, b) in ((wb, wbr, bb), (wc, wcr, bc), (wx, wxr, bx)):
                    for k in range(KC):
                        mm(bank(b), wt[:, k, :], U[:, k, :], k == 0, k == KC - 1,
                           [(wr, "r"), (("U", k), "r"), (breg(b), "w")])
                if not prompt:
                    bh = nextbank()
                    for n2, (wt, wr) in enumerate(((wc, wcr), (wx, wxr))):
                        for k in range(KC):
                            mm(bank(bh)[:, 2 * n2:2 * n2 + 2], wt[:, k, :], halo_cols(UB, k, i), k == 0, k == KC - 1,
                               [(wr, "r"), ("UB", "r"), (breg(bh), "w")])
                act(T[0][:], bank(bx), AF.Copy, [(breg(bx), "x"), (("T", 0), "w")])
                tt(xc, v512(bank(bc)), v512(T[0][:]), ALU.mult, [(breg(bc), "x"), (("T", 0), "r"), (("XBV", 0), "w")])
                if not prompt:
                    act(T[1][:, 0:2], bank(bh)[:, 2:4], AF.Copy, [(breg(bh), "x"), (("T", 1), "w")])
                    tt(edge_cols(XB), bank(bh)[:, 0:2], T[1][:, 0:2], ALU.mult,
                       [(breg(bh), "x"), (("T", 1), "r"), (("XBV", 0), "w")])
                y = v512(T[2][:])
                yreg = ("T", 2)
                ts(y, xl, convw[:, c, 0:1], None, ALU.mult, None, [(("XBV", 0), "r"), ("convw", "r"), (yreg, "w")])
                stt(y, xc, convw[:, c, 1:2], y, ALU.mult, ALU.add, [(("XBV", 0), "r"), ("convw", "r"), (yreg, "r"), (yreg, "w")])
                stt(y, xr, convw[:, c, 2:3], y, ALU.mult, ALU.add, [(("XBV", 0), "r"), ("convw", "r"), (yreg, "r"), (yreg, "w")])
                tt(CONV(c), T[2][:], bank(bb), ALU.mult, [(yreg, "r"), (breg(bb), "x"), (("G", 8 + c), "w")])

        def conv_chunk_gen(i, c):
            xc, xl, xr = XB[:, 1:513], XB[:, 0:512], XB[:, 2:514]
            xreg = ("XBV", 0)
            ct0, ct1, ct2 = MO[:, 0, :], MO[:, 1, :], MO[:, 2, :]
            wc, wcr = load_w1(w_in, C_C + c * 128)
            wx, wxr = load_w1(w_in, C_X + c * 128)
            wb, wbr = load_w1(w_in, C_B + c * 128)
            for k in range(KC):
                mm(bank(0), wc[:, k, :], U[:, k, :], k == 0, k == KC - 1, [(wcr, "r"), (("U", k), "r"), (breg(0), "w")])
                if k == 3:
                    yield
            yield
            for k in range(KC):
                mm(bank(1), wx[:, k, :], U[:, k, :], k == 0, k == KC - 1, [(wxr, "r"), (("U", k), "r"), (breg(1), "w")])
                if k == 3:
                    yield
            yield
            cp(ct0, bank(1), [(breg(1), "x"), (("MO", 0), "w")])
            tt(xc, bank(0), ct0, ALU.mult, [(breg(0), "x"), (("MO", 0), "r"), (xreg, "w")])
            yield
            for n2, (wt, wr) in enumerate(((wc, wcr), (wx, wxr))):
                for k in range(KC):
                    mm(bank(0)[:, 2 * n2:2 * n2 + 2], wt[:, k, :], halo_cols(UB, k, i), k == 0, k == KC - 1,
                       [(wr, "r"), ("UB", "r"), (breg(0), "w")])
            for k in range(KC):
                mm(bank(1), wb[:, k, :], U[:, k, :], k == 0, k == KC - 1, [(wbr, "r"), (("U", k), "r"), (breg(1), "w")])
                if k == 3:
                    yield
            yield
            cp(ct1[:, 0:2], bank(0)[:, 2:4], [(breg(0), "x"), (("MO", 1), "w")])
            tt(edge_cols(XB), bank(0)[:, 0:2], ct1[:, 0:2], ALU.mult, [(breg(0), "x"), (("MO", 1), "r"), (xreg, "w")])
            yacc = [(xreg, "r"), ("convw", "r"), (("MO", 2), "r"), (("MO", 2), "w")]
            ts(ct2, xl, convw[:, c, 0:1], None, ALU.mult, None, [(xreg, "r"), ("convw", "r"), (("MO", 2), "w")])
            stt(ct2, xc, convw[:, c, 1:2], ct2, ALU.mult, ALU.add, yacc)
            stt(ct2, xr, convw[:, c, 2:3], ct2, ALU.mult, ALU.add, yacc)
            tt(CONV(c), ct2, bank(1), ALU.mult, [(("MO", 2), "r"), (breg(1), "x"), (("G", 8 + c), "w")])

        def merge():
            for oc in range(8):
                wa, war = load_w1(w_att, oc * 128)
                wcv, wcvr = load_w1(w_cvo, oc * 128)
                wga, wgar = load_w1(w_in, C_GA + oc * 128)
                wgc, wgcr = load_w1(w_in, C_GC + oc * 128)
                ba, bcv, bga, bgc = nextbank(), nextbank(), nextbank(), nextbank()
                for k in range(KC):
                    mm(bank(ba), wa[:, k, :], ATT(k), k == 0, k == KC - 1, [(war, "r"), (("G", k), "r"), (breg(ba), "w")])
                for k in range(KC):
                    mm(bank(bcv), wcv[:, k, :], CONV(k), k == 0, k == KC - 1,
                       [(wcvr, "r"), (("G", 8 + k), "r"), (breg(bcv), "w")])
                for (wt, wr, b) in ((wga, wgar, bga), (wgc, wgcr, bgc)):
                    for k in range(KC):
                        mm(bank(b), wt[:, k, :], U[:, k, :], k == 0, k == KC - 1,
                           [(wr, "r"), (("U", k), "r"), (breg(b), "w")])
                act(T[0][:], bank(bga), AF.Sigmoid, [(breg(bga), "x"), (("T", 0), "w")])
                act(T[1][:], bank(bgc), AF.Sigmoid, [(breg(bgc), "x"), (("T", 1), "w")])
                tt(T[2][:], bank(ba), T[0][:], ALU.mult, [(breg(ba), "x"), (("T", 0), "r"), (("T", 2), "w")])
                tt(T[3][:], bank(bcv), T[1][:], ALU.mult, [(breg(bcv), "x"), (("T", 1), "r"), (("T", 3), "w")])
                tt(MERGED(oc), T[2][:], T[3][:], ALU.add, [(("T", 2), "r"), (("T", 3), "r"), (("G", 16 + oc), "w")])

        RS2 = XBV[1][:, 0, 0:TT]
        TP2 = XBV[1][:, 1, 0:TT]
        SQ2 = [XBV[2][:, 0, :].bitcast(BF16)[:, 0:TT], XBV[2][:, 0, :].bitcast(BF16)[:, TT:2 * TT]]

        def w_o_proj_gen(fold_stats=False):
            sb2 = None
            if fold_stats:
                sb2 = nextbank()
                reserved.add(sb2)

            def stat_mm(oc):
                mm(bank(sb2), ones_bf[:], SQ2[oc % 2], oc == 0, oc == 7,
                   [(("XBV", 2), "r"), ("ones", "r"), (breg(sb2), "w")])

            for oc in range(8):
                wt, wr = load_w1(w_o, oc * 128)
                b = nextbank()
                for k in range(KC):
                    mm(bank(b), wt[:, k, :], MERGED(k), k == 0, k == KC - 1,
                       [(wr, "r"), (("G", 16 + k), "r"), (breg(b), "w")])
                    if k == 3:
                        reserved.add(b)
                        yield
                        reserved.discard(b)
                if fold_stats and oc >= 1:
                    stat_mm(oc - 1)
                act(MO[:, oc, :], bank(b), AF.Copy, [(breg(b), "x"), (("MO", oc), "w")])
                if fold_stats:
                    act(SQ2[oc % 2], MO[:, oc, :], AF.Square, [(("MO", oc), "r"), (("XBV", 2), "w")])
                yield
            if fold_stats:
                stat_mm(7)
                act(RS2, bank(sb2), AF.Ln, [(breg(sb2), "x"), (("XBV", 1), "w")], bias=EPS, scale=1.0 / D)
                act(RS2, RS2, AF.Exp, [(("XBV", 1), "r"), (("XBV", 1), "w")], scale=-0.5)
                reserved.discard(sb2)

        def post_apply_gen(slot, which, grp):
            for k in range(KC):
                stt(TP2, MO[:, k, :], modcol(3 * which + 2, k, grp), RS2, ALU.mult, ALU.mult,
                    [(("MO", k), "r"), ("MOD", "r"), (("XBV", 1), "r"), (("XBV", 1), "w")])
                tt(hslice(slot, k), hslice(slot, k), TP2, ALU.add,
                   [(("hT", slot, k), "r"), (("XBV", 1), "r"), (("hT", slot, k), "w")])
                yield

        def w_o_proj():
            run(w_o_proj_gen())

        def nextpair():
            while True:
                b = bank_rr[0]
                if b % 2:
                    b = (b + 1) % 8
                bank_rr[0] = (b + 2) % 8
                if b not in reserved and b + 1 not in reserved:
                    return b // 2

        def ffn_up_gen(prompt, i):
            v512, xviews = tile_views(prompt)

            def bufs(f):
                st = f % 3
                return (XBV[st], ("XBV", st), FT[3 * st], FT[3 * st + 1], FT[3 * st + 2],
                        FTR[3 * st], FTR[3 * st + 1], FTR[3 * st + 2])

            def front(f):
                xbv, xreg, tg, tv, tsl, rg, rv, rsl = bufs(f)
                gc, gl, gr = xviews(xbv[:, 0, :])
                vc, vl, vr = xviews(xbv[:, 1, :])
                wg, wgr = load_w1(w_up, f * 128)
                wv, wvr = load_w1(w_up, DFF + f * 128)
                d = nextpair()
                bg, bv = 2 * d, 2 * d + 1
                for (wt, wr, b) in ((wg, wgr, bg), (wv, wvr, bv)):
                    for k in range(KC):
                        mm(bank(b), wt[:, k, :], U[:, k, :], k == 0, k == KC - 1,
                           [(wr, "r"), (("U", k), "r"), (breg(b), "w")])
                if not prompt:
                    bh = nextbank()
                    for n2, (wt, wr) in enumerate(((wg, wgr), (wv, wvr))):
                        for k in range(KC):
                            mm(bank(bh)[:, 2 * n2:2 * n2 + 2], wt[:, k, :], halo_cols(U2B, k, i), k == 0, k == KC - 1,
                               [(wr, "r"), ("U2B", "r"), (breg(bh), "w")])
                act(v512(tg), v512(bank(bg)), AF.Identity, [(breg(bg), "x"), ("convf", "r")] + [(r, "w") for r in rg],
                    scale=convf[:, f, 1:2])
                act(v512(tv), v512(bank(bv)), AF.Identity, [(breg(bv), "x"), ("convf", "r")] + [(r, "w") for r in rv],
                    scale=convf[:, NFC + f, 1:2])
                if prompt:
                    act(gc, v512(bank(bg)), AF.Copy, [(breg(bg), "x"), (xreg, "w")])
                    act(vc, v512(bank(bv)), AF.Copy, [(breg(bv), "x"), (xreg, "w")])
                else:
                    act(xbv[:, :, 1:513], PS[d][:], AF.Copy, [(breg(bg), "x"), (breg(bv), "x"), (xreg, "w")])
                    act(edge_cols(xbv[:, 0, :]), bank(bh)[:, 0:2], AF.Copy, [(breg(bh), "x"), (xreg, "w")])
                    act(edge_cols(xbv[:, 1, :]), bank(bh)[:, 2:4], AF.Copy, [(breg(bh), "x"), (xreg, "w")])

            def taps(f):
                xbv, xreg, tg, tv, tsl, rg, rv, rsl = bufs(f)
                gc, gl, gr = xviews(xbv[:, 0, :])
                vc, vl, vr = xviews(xbv[:, 1, :])
                for (l, r, tcol, tbuf, treg) in ((gl, gr, f, tg, rg), (vl, vr, NFC + f, tv, rv)):
                    y = v512(tbuf)
                    tacc = [(xreg, "r"), ("convf", "r")] + [(r_, "r") for r_ in treg] + [(r_, "w") for r_ in treg]
                    stt(y, l, convf[:, tcol, 0:1], y, ALU.mult, ALU.add, tacc)
                    stt(y, r, convf[:, tcol, 2:3], y, ALU.mult, ALU.add, tacc)

            def silu(f):
                xbv, xreg, tg, tv, tsl, rg, rv, rsl = bufs(f)
                act(tsl, tg, AF.Silu, [(r, "r") for r in rg] + [(r, "w") for r in rsl])

            def fin(f):
                xbv, xreg, tg, tv, tsl, rg, rv, rsl = bufs(f)
                tt(ACTF(f), tsl, tv, ALU.mult, [(r, "r") for r in rsl] + [(r, "r") for r in rv] + [(("G", f), "w")])

            for f in range(NFC):
                front(f)
                if f >= 1:
                    silu(f - 1)
                taps(f)
                if f >= 1:
                    fin(f - 1)
                yield
            silu(NFC - 1)
            fin(NFC - 1)

        def ffn_down_gen():
            wdv = w_dn.rearrange("(k p) c -> p k c", p=128)
            for half in range(2):
                bks = [nextbank() for _ in range(4)]
                reserved.update(bks)
                for kf2 in range(NFC // 2):
                    wt, wr = load_w([(0, wdv[:, 2 * kf2:2 * kf2 + 2, half * 512:(half + 1) * 512])])
                    wflat = wt[:].rearrange("p k c -> p (k c)")
                    for kk in range(2):
                        kf = 2 * kf2 + kk
                        for q_ in range(4):
                            mm(bank(bks[q_]), wflat[:, kk * 512 + q_ * 128:kk * 512 + (q_ + 1) * 128], ACTF(kf),
                               kf == 0, kf == NFC - 1, [(wr, "r"), (("G", kf), "r"), (breg(bks[q_]), "w")])
                    yield
                reserved.difference_update(bks)
                for q_ in range(4):
                    oc = half * 4 + q_
                    act(MO[:, oc, :], bank(bks[q_]), AF.Copy, [(breg(bks[q_]), "x"), (("MO", oc), "w")])
                yield

        def store_out_gen(slot, y_rows):
            stg = MOs.rearrange("p (b d) -> p b d", d=D)
            for blk in range(4):
                for half in range(2):
                    b = nextbank()
                    for kk in range(4):
                        k = half * 4 + kk
                        tr(bank(b)[:, kk * 128:(kk + 1) * 128], hslice(slot, k, blk * 128, 128), ident[:],
                           [(("hT", slot, k), "r"), ("ident", "r"), (breg(b), "w")])
                    reserved.add(b)
                    yield
                    reserved.discard(b)
                    act(stg[:, blk, half * 512:(half + 1) * 512], bank(b), AF.Copy,
                        [(breg(b), "x"), (("MO", 2 * blk + half), "w")])
                dma("sp", y_rows[blk * 128:(blk + 1) * 128, :], stg[:, blk, :],
                    [(("MO", 2 * blk), "r"), (("MO", 2 * blk + 1), "r")], ("st", 10 + blk), is_store=True)
            yield

        def store_out(slot, y_rows):
            run(store_out_gen(slot, y_rows))

        def edge_prenorm(slot, i, grp):
            a0 = hT[:, 0, slot * TT:slot * TT + 1]
            cols = AP(a0.tensor, a0.offset, [list(a0.ap[0]), [NS, KC], [TT - 1, 2]])
            hregs = [(("hT", slot, k), "r") for k in range(KC)]
            act(SQe[:], cols, AF.Square, hregs + [("SQe", "w")])
            b = nextbank()
            for k in range(KC):
                mm(bank(b)[:, 0:2], ones_bf[:], SQe[:, k, :], k == 0, k == KC - 1,
                   [("SQe", "r"), ("ones", "r"), (breg(b), "w")])
            act(RSe[:], bank(b)[:, 0:2], AF.Ln, [(breg(b), "x"), ("RSe", "w")], bias=EPS, scale=1.0 / D)
            act(RSe[:], RSe[:], AF.Exp, [("RSe", "r"), ("RSe", "w")], scale=-0.5)
            r0 = RSe[:, 0:1]
            rsb = AP(r0.tensor, r0.offset, [list(r0.ap[0]), [0, KC], [1, 2]])
            m0 = MOD[:, 3, 0, grp:grp + 1]
            a2b = AP(m0.tensor, m0.offset, [list(m0.ap[0]), [2, KC], [0, 2]])
            m1 = MOD[:, 4, 0, grp:grp + 1]
            b2b = AP(m1.tensor, m1.offset, [list(m1.ap[0]), [2, KC], [0, 2]])
            tt(Te[:], cols, rsb, ALU.mult, hregs + [("RSe", "r"), ("Te", "w")])
            tt(Te[:], Te[:], a2b, ALU.mult, [("Te", "r"), ("MOD", "r"), ("Te", "w")])
            tt(U2B[:, :, 1 + 2 * i:3 + 2 * i], Te[:], b2b, ALU.add, [("Te", "r"), ("MOD", "r"), ("U2B", "w")])

        def save_edges(dst, i):
            a = U[:, :, 0:1]
            src = AP(a.tensor, a.offset, [list(a.ap[0]), list(a.ap[1]), [511, 2]])
            cp(dst[:, :, 1 + 2 * i:3 + 2 * i], src, [(("U", k), "r") for k in range(KC)] + [("UB" if dst is UB else "U2B", "w")])

        P.stage = "pre_norm"; pre_norm(0, 0, 1)
        P.stage = "kv_stage"; kv_stage(0, 0, False, True)
        P.stage = "attention"; attention(True)
        P.stage = "conv_mixer"; conv_mixer(True, 0)
        P.stage = "merge"; merge()
        P.stage = "w_o_proj"; w_o_proj()
        P.stage = "post_norm_residual"; post_norm_residual(0, 0, 1)
        P.stage = "pre_norm"; pre_norm(0, 1, 1)
        P.stage = "ffn"; run(ffn_up_gen(True, 0)); run(ffn_down_gen())
        P.stage = "post_norm_residual"; post_norm_residual(0, 1, 1)
        P.stage = "store_out"; store_out(0, y_p)
        memset(V[:, :, :, 64:128], 1.0, [(("V", b_), "w") for b_ in range(20)])

        P.stage = "cache"
        ckbuf = [XBV[1 + h_][:].rearrange("p h n -> p (h n)")[:, 0:1024].rearrange("p (b g u e) -> p b g u e", b=2, g=4, u=2)
                 for h_ in range(2)]
        ckv = ck.rearrange("(b p) (g e) -> p b g e", p=128, e=64)
        for u in range(2):
            for blk in range(4):
                dma("sp", ckbuf[blk // 2][:, blk % 2, :, u, :], ckv[:, blk, :, :], [(("XBV", 1 + blk // 2), "w")], "cache")
        cvv = cv.rearrange("(b p) (g e) -> p b g e", p=128, e=64)
        for blk in range(4):
            dma("pool", V[:, blk, :, 0:64], cvv[:, blk, :, :], [(("V", blk), "w")], "cachev")
        for g in range(4):
            b = nextbank()
            for blk in range(4):
                tr(bank(b)[:, blk * 128:(blk + 1) * 128],
                   ckbuf[blk // 2][:, blk % 2, g, :, :].rearrange("p u e -> p (u e)"), ident[:],
                   [(("XBV", 1), "r"), (("XBV", 2), "r"), ("ident", "r"), (breg(b), "w")])
            act(KTv[:, g, 0:PAST], bank(b), AF.Copy, [(breg(b), "x"), (("KT", g), "w")])
        for i in range(4):
            if i == 0:
                P.stage = "load_xT"; load_xT(x_s[0:TT, :], 0)
            P.stage = "pre_norm"; pre_norm(i, 0, 0)
            P.stage = "save_edges"; save_edges(UB, i)
            P.stage = "load_rope"; load_rope(i)
            P.stage = "kv_stage"; kv_stage(PAST + i * TT, 4 + 4 * i, True, False)
        P.stage = "pre_norm"; pre_norm(0, 0, 0)

        for i in range(4):
            P.stage = "load_rope"; load_rope(i)
            P.stage = "attention"; attention(False, conv_i=i, pre_side=post_apply_gen(i - 1, 0, 0) if i > 0 else None)
            if i > 0:
                P.stage = "save_edges"; edge_prenorm(i - 1, i - 1, 0)
            P.stage = "merge"; merge()
            P.stage = "w_o_proj"; interleave(w_o_proj_gen(fold_stats=True),
                                              pre_norm_gen(i + 1, 0, 0, dve_sq=True) if i < 3
                                              else pre_norm_gen(0, 1, 0, dve_sq=True), 1)
        P.stage = "post_norm_residual"; run(post_apply_gen(3, 0, 0))
        P.stage = "save_edges"; edge_prenorm(3, 3, 0)
        for i in range(4):
            P.stage = "ffn"
            side = post_norm_gen(i - 1, 1, 0) if i > 0 else iter(())
            interleave(ffn_up_gen(False, i), side, 1)
            side2 = chain(store_out_gen(i - 1, y_s[(i - 1) * TT:i * TT, :]) if i > 0 else iter(()),
                          pre_norm_gen(i + 1, 1, 0) if i < 3 else iter(()))
            interleave(ffn_down_gen(), side2, 1)
        P.stage = "post_norm_residual"; post_norm_residual(3, 1, 0)
        P.stage = "store_out"; store_out(3, y_s[3 * TT:4 * TT, :])

        P.emit(nc, es)
    build_program.last_prog = P
    return nc


def _rope_consts():
    t = np.arange(NS)
    row = (t // GRID_W).astype(np.float32)
    col = (t % GRID_W).astype(np.float32)
    inv = np.power(np.float32(10000.0), -np.arange(0, 32, 2, dtype=np.float32) / np.float32(32)).astype(np.float32)
    cos = np.zeros((128, NS), np.float32)
    sin = np.zeros((128, NS), np.float32)
    for p in range(128):
        d = p % 64
        pos = row if d < 32 else col
        ang = (pos * inv[d % 16]).astype(np.float32)
        cos[p] = np.cos(ang)
        sin[p] = np.sin(ang)
    pm = np.zeros((128, 128), np.float32)
    for m in range(128):
        j = m % 32
        if j < 16:
            pm[m + 16, m] = -1.0
        else:
            pm[m - 16, m] = 1.0
    return cos, sin, pm


def _col_layout(v):
    return np.ascontiguousarray(np.asarray(v, np.float32).reshape(-1, 128).T)


_NC_CACHE = {}


def kernel(x_prompt, x_sample, cache_k, cache_v, c, c_ctx, w_ada, b_ada, g_pre1, g_post1, g_pre2, g_post2,
           w_in, q_norm, k_norm, w_att_out, conv_w, w_conv_out, w_o, w_up, conv_ffn, w_down):
    f = lambda a: np.ascontiguousarray(np.asarray(a, dtype=np.float32))
    x_prompt, x_sample, cache_k, cache_v = f(x_prompt), f(x_sample), f(cache_k), f(cache_v)
    n = 8
    cos, sin, pm = _rope_consts()
    shared = {
        "bada": _col_layout(f(b_ada)[0]),
        "gvec": np.ascontiguousarray(np.stack([_col_layout(f(g)[0]) for g in (g_pre1, g_post1, g_pre2, g_post2)],
                                              axis=1).reshape(128, 32)),
        "qkg": np.ascontiguousarray(np.stack([np.tile(f(q_norm)[0], 2), np.tile(f(k_norm)[0], 2)], axis=1)),
        "convw": np.ascontiguousarray(np.stack([_col_layout(f(conv_w)[0, j]) for j in range(3)], axis=2).reshape(128, 24)),
        "convf": np.ascontiguousarray(np.stack([_col_layout(f(conv_ffn)[0, j]) for j in range(3)], axis=2).reshape(128, 132)),
        "ident": np.eye(128, dtype=np.float32),
        "pmat": pm, "cos": cos, "sin": sin,
        "w_ada": f(w_ada)[0], "w_in": f(w_in)[0], "w_att": f(w_att_out)[0], "w_cvo": f(w_conv_out)[0],
        "w_o": f(w_o)[0], "w_up": f(w_up)[0], "w_dn": f(w_down)[0],
    }
    cc = _col_layout(f(c_ctx))
    in_maps = []
    for i in range(n):
        m = dict(shared)
        m["x_s"] = x_sample[i]
        m["x_p"] = x_prompt[2 * i:2 * i + 2].reshape(NP, D)
        m["ck"] = cache_k[i, 0].reshape(PAST, 256)
        m["cv"] = cache_v[i, 0].reshape(PAST, 256)
        m["c2"] = np.ascontiguousarray(np.stack([_col_layout(f(c)[i]), cc], axis=2).reshape(128, 16))
        in_maps.append(m)
    if "nc" not in _NC_CACHE:
        _NC_CACHE["nc"] = build_program()
    res = run_bass_kernel_spmd(_NC_CACHE["nc"], in_maps, core_ids=list(range(n)))
    y_prompt = np.empty((16, 256, D), np.float32)
    y_sample = np.empty((8, NS, D), np.float32)
    new_k = np.empty((16, 1, 256, 4, 64), np.float32)
    new_v = np.empty((16, 1, 256, 4, 64), np.float32)
    for i, r in enumerate(res.results):
        y_prompt[2 * i:2 * i + 2] = np.asarray(r["y_p"]).reshape(2, 256, D)
        y_sample[i] = np.asarray(r["y_s"])
        new_k[2 * i:2 * i + 2, 0] = np.asarray(r["nk"]).reshape(2, 256, 4, 64)
        new_v[2 * i:2 * i + 2, 0] = np.asarray(r["nv"]).reshape(2, 256, 4, 64)
    return (y_prompt, y_sample, new_k, new_v)
```

```python
import numpy as np
from contextlib import ExitStack

import concourse.bass as bass
import concourse.mybir as mybir
from concourse.ap import AP
from concourse.bass_utils import run_bass_kernel_spmd

F32 = mybir.dt.float32
BF16 = mybir.dt.bfloat16
AF = mybir.ActivationFunctionType
ALU = mybir.AluOpType

D = 1024
KC = 8
TT = 512
NS = 2048
NP = 512
PAST = 512
NKEY = PAST + NS
DFF = 2816
NFC = 22
IN_W = 6656
EPS = 1e-6
C_Q, C_K, C_V, C_B, C_C, C_X, C_GA, C_GC = 0, 1024, 1280, 1536, 2560, 3584, 4608, 5632
N_WSLOT = 6
GRID_W = 64


class _Op:
    __slots__ = ("eng", "emit", "waits", "tok", "dma_sem", "signal", "stage")

    def __init__(self, eng, emit, waits, tok, dma_sem, stage):
        self.eng, self.emit, self.waits, self.tok, self.dma_sem = eng, emit, waits, tok, dma_sem
        self.signal = False
        self.stage = stage


class Prog:
    ENGS = ("pe", "act", "dve", "pool", "sp")

    def __init__(self):
        self.streams = {e: [] for e in self.ENGS}
        self.reg = {}
        self.dma_cnt = {}
        self.batch = set()
        self.store_sems = set()
        self.stage = "init"

    def op(self, eng, emit, acc, dma_sem=None, is_store=False):
        stream = self.streams[eng]
        if dma_sem is not None:
            cnt = self.dma_cnt.get(dma_sem, 0) + 16
            self.dma_cnt[dma_sem] = cnt
            tok = ("d", dma_sem, cnt)
            if is_store:
                self.store_sems.add(dma_sem)
        else:
            tok = ("e", eng, len(stream))
        rkey = tok[1] if tok[0] == "e" else tok
        deps = {}
        for reg, mode in acc:
            st = self.reg.get(reg)
            if st is None:
                st = [None, {}]
                self.reg[reg] = st
            lw, readers = st
            if mode == "r":
                if lw is not None:
                    deps[lw] = True
            elif mode == "x":
                if lw is not None:
                    deps[lw] = True
                for r in readers.values():
                    deps.setdefault(r, False)
            else:
                if lw is not None:
                    deps.setdefault(lw, False)
                for r in readers.values():
                    deps.setdefault(r, False)
        for reg, mode in acc:
            st = self.reg[reg]
            if mode == "r":
                st[1][rkey] = tok
            elif mode == "x":
                st[1] = {rkey: tok}
        for reg, mode in acc:
            if mode == "w":
                st = self.reg[reg]
                st[0] = tok
                st[1] = {}
        waits = []
        for t, raw in deps.items():
            if t == tok:
                continue
            if t[0] == "d" and dma_sem is not None and t[1] == dma_sem:
                continue
            if t[0] == "e" and t[1] == eng:
                if eng in ("pe", "sp"):
                    continue
            if t[0] == "e":
                self.streams[t[1]][t[2]].signal = True
            waits.append(t)
        stream.append(_Op(eng, emit, waits, tok, dma_sem, self.stage))

    def emit(self, nc, es):
        for e in self.ENGS:
            for o in self.streams[e]:
                o.signal = False
        for eng in self.ENGS:
            waited = {}
            for o in self.streams[eng]:
                pend = {}
                for t in o.waits:
                    if t[0] == "e":
                        key, v = ("e", t[1]), t[2]
                    else:
                        key = ("d", t[1])
                        v = self.dma_cnt[t[1]] if t[1] in self.batch else t[2]
                    if waited.get(key, -1) >= v:
                        continue
                    waited[key] = v
                    pend[key] = t
                o.waits = list(pend.values())
                for t in o.waits:
                    if t[0] == "e":
                        self.streams[t[1]][t[2]].signal = True
        sig = {}
        for e in self.ENGS:
            c = 0
            arr = []
            for o in self.streams[e]:
                if o.dma_sem is None and o.signal:
                    c += 1
                arr.append(c)
            sig[e] = arr
        esem = {e: es.enter_context(nc.semaphore("sem_" + e)) for e in ("pe", "act", "dve", "pool")}
        dsem = {}
        for i, k in enumerate(self.dma_cnt):
            dsem[k] = es.enter_context(nc.semaphore("dsem%d" % i))
        block = es.enter_context(nc.Block())
        hooks = {"pe": block.tensor, "act": block.scalar, "dve": block.vector, "pool": block.gpsimd, "sp": block.sync}
        for eng in self.ENGS:
            ops = self.streams[eng]

            def body(e, ops=ops, eng=eng):
                for o in ops:
                    pend = []
                    for t in o.waits:
                        if t[0] == "e":
                            pend.append((esem[t[1]], sig[t[1]][t[2]]))
                        else:
                            pend.append((dsem[t[1]], self.dma_cnt[t[1]] if t[1] in self.batch else t[2]))
                    attach = pend.pop() if (pend and o.dma_sem is None and eng in ("pe", "act", "dve")) else None
                    for sem, val in pend:
                        e.wait_ge(sem, val)
                    ins = o.emit(e)
                    if attach is not None:
                        ins._wait_ge(attach[0], attach[1])
                    if o.dma_sem is not None:
                        ins.then_inc(dsem[o.dma_sem], 16)
                    elif o.signal:
                        ins.then_inc(esem[eng], 1)
                if eng == "sp":
                    for k in self.store_sems:
                        e.wait_ge(dsem[k], self.dma_cnt[k])

            hooks[eng](body)


def build_program():
    nc = bass.Bass("TRN2", target_bir_lowering=False)

    def din(name, shape):
        return nc.dram_tensor(name, list(shape), F32, kind="ExternalInput").ap()

    def dout(name, shape):
        return nc.dram_tensor(name, list(shape), F32, kind="ExternalOutput").ap()

    x_s = din("x_s", [NS, D])
    x_p = din("x_p", [NP, D])
    ck = din("ck", [PAST, 256])
    cv = din("cv", [PAST, 256])
    c2_d = din("c2", [128, KC * 2])
    bada_d = din("bada", [128, 48])
    gvec_d = din("gvec", [128, 4 * KC])
    qkg_d = din("qkg", [128, 2])
    convw_d = din("convw", [128, KC * 3])
    convf_d = din("convf", [128, 44 * 3])
    ident_d = din("ident", [128, 128])
    pmat_d = din("pmat", [128, 128])
    cos_d = din("cos", [128, NS])
    sin_d = din("sin", [128, NS])
    w_ada = din("w_ada", [D, 6 * D])
    w_in = din("w_in", [D, IN_W])
    w_att = din("w_att", [D, D])
    w_cvo = din("w_cvo", [D, D])
    w_o = din("w_o", [D, D])
    w_up = din("w_up", [D, 2 * DFF])
    w_dn = din("w_dn", [DFF, D])
    y_s = dout("y_s", [NS, D])
    y_p = dout("y_p", [NP, D])
    nk_o = dout("nk", [NP, 256])
    nv_o = dout("nv", [NP, 256])

    es = ExitStack()
    with es:
        def sb(name, shape, dt):
            return es.enter_context(nc.sbuf_tensor("sb_" + name, list(shape), dt))

        hT = sb("hT", [128, KC, NS], F32)
        KT = sb("KT", [128, 4 * NKEY], BF16)
        V = sb("V", [128, 20, 4, 128], BF16)
        UB = sb("UB", [128, KC, 10], BF16)
        U2B = sb("U2B", [128, KC, 10], BF16)
        U = sb("U", [128, KC, TT], BF16)
        SQ = [sb("SQ%d" % i, [128, TT], BF16) for i in range(2)]
        RS = sb("RS", [128, TT], F32)
        T = [sb("T%d" % i, [128, TT], F32) for i in range(4)]
        QT = [[sb("QT%d_%d" % (i, h), [128, TT], BF16) for h in range(2)] for i in range(2)]
        PT = [sb("PT%d" % i, [128, 2, TT], BF16) for i in range(3)]
        G24 = sb("G24", [128, 24, TT], BF16)
        MO = sb("MO", [128, KC, TT], F32)
        XBV = [sb("XBV%d" % i, [128, 2, 516], F32) for i in range(3)]
        XB = XBV[0][:, 0, :]
        WS = [sb("WS%d" % i, [128, KC, 128], BF16) for i in range(N_WSLOT)]
        COS = sb("COS", [128, TT], F32)
        SIN = sb("SIN", [128, TT], F32)
        ident = sb("ident", [128, 128], F32)
        pmat = sb("pmat", [128, 128], F32)
        ones_bf = sb("ones_bf", [128, 128], BF16)
        oblk_bf = sb("oblk_bf", [128, 128], BF16)
        c2 = sb("c2", [128, KC, 2], F32)
        sc2 = sb("sc2", [128, KC, 2], F32)
        sc2b = sb("sc2b", [128, KC, 2], BF16)
        SQe = sb("SQe", [128, KC, 2], BF16)
        RSe = sb("RSe", [128, 2], F32)
        Te = sb("Te", [128, KC, 2], F32)
        bada = sb("bada", [128, 48], F32)
        adaT = sb("adaT", [128, 48, 2], F32)
        gvec = sb("gvec", [128, 4, KC], F32)
        MOD = sb("MOD", [128, 6, KC, 2], F32)
        qkg = sb("qkg", [128, 2], F32)
        convw = sb("convw", [128, KC, 3], F32)
        convf = sb("convf", [128, 44, 3], F32)
        PS = [es.enter_context(nc.psum_tensor("PS%d" % i, [128, 2, TT], F32)) for i in range(4)]

        KTv = KT[:].rearrange("p (g n) -> p g n", g=4)
        Vf = V[:].rearrange("p b g e -> p (b g e)").bitcast(F32)
        FT = [Vf[:, j * TT:(j + 1) * TT] for j in range(10)]
        FTR = [[("V", 2 * j), ("V", 2 * j + 1)] for j in range(10)]
        MOs = MO[:].rearrange("p k t -> p (k t)")
        ATT = lambda c: G24[:, c, :]
        CONV = lambda c: G24[:, 8 + c, :]
        MERGED = lambda c: G24[:, 16 + c, :]
        ACTF = lambda f: G24[:, f, :]

        P = Prog()
        P.batch.update(["const", "cache", "cachev"])

        def bank(b):
            return PS[b // 2][:, b % 2, :]

        def breg(b):
            return ("ps", b)

        bank_rr = [0]
        reserved = set()

        def nextbank():
            while True:
                b = bank_rr[0]
                bank_rr[0] = (b + 1) % 8
                if b not in reserved:
                    return b

        def mm(out, lhsT, rhs, start, stop, acc):
            P.op("pe", lambda e: e.matmul(out, lhsT=lhsT, rhs=rhs, start=start, stop=stop), acc)

        def tr(out, in_, idn, acc):
            P.op("pe", lambda e: e.transpose(out=out, in_=in_, identity=idn), acc)

        def act(out, in_, func, acc, bias=None, scale=None):
            kw = {}
            if bias is not None:
                kw["bias"] = bias
            if scale is not None:
                kw["scale"] = scale
            P.op("act", lambda e: e.activation(out=out, in_=in_, func=func, **kw), acc)

        def tt(out, in0, in1, op, acc, eng="dve"):
            P.op(eng, lambda e: e.tensor_tensor(out=out, in0=in0, in1=in1, op=op), acc)

        def ts(out, in0, s1, s2, op0, op1, acc, eng="dve"):
            if s2 is None:
                P.op(eng, lambda e: e.tensor_scalar(out=out, in0=in0, scalar1=s1, scalar2=None, op0=op0), acc)
            else:
                P.op(eng, lambda e: e.tensor_scalar(out=out, in0=in0, scalar1=s1, scalar2=s2, op0=op0, op1=op1), acc)

        def stt(out, in0, scalar, in1, op0, op1, acc, eng="dve"):
            P.op(eng, lambda e: e.scalar_tensor_tensor(out=out, in0=in0, scalar=scalar, in1=in1, op0=op0, op1=op1), acc)

        def cp(out, in_, acc, eng="dve"):
            P.op(eng, lambda e: e.tensor_copy(out=out, in_=in_), acc)

        def recip(out, in_, acc):
            P.op("dve", lambda e: e.reciprocal(out=out, in_=in_), acc)

        def memset(ap, val, acc, eng="dve"):
            P.op(eng, lambda e: e.memset(ap, val), acc)

        def dma(q, out, in_, acc, sem, is_store=False):
            P.op(q, lambda e: e.dma_start(out=out, in_=in_), acc, dma_sem=sem, is_store=is_store)

        wslot_rr = [0]

        def load_w(srcs):
            s = wslot_rr[0]
            wslot_rr[0] = (s + 1) % N_WSLOT
            for c0, src in srcs:
                kcn, ncol = src.shape[1], src.shape[2]
                if ncol > 128:
                    dst = WS[s][:].rearrange("p k c -> p (k c)")[:, 0:kcn * ncol].rearrange("p (k c) -> p k c", c=ncol)
                else:
                    dst = WS[s][:, 0:kcn, c0:c0 + ncol]
                dma("pool", dst, src, [(("ws", s), "w")], ("w", s))
            return WS[s], ("ws", s)

        def wcols(w, c0, ncol=128):
            return w.rearrange("(k p) c -> p k c", p=128)[:, :, c0:c0 + ncol]

        def load_w1(w, c0):
            return load_w([(0, wcols(w, c0))])

        WD = [KT[:, j * 2816:(j + 1) * 2816].rearrange("p (k c) -> p k c", c=128) for j in range(3)]
        WD_KT = [(0, 1), (1, 2), (2, 3)]
        wd_rr = [0]

        def load_wd(oc):
            j = wd_rr[0]
            wd_rr[0] = (j + 1) % 3
            regs = [("wd", j)] + [("KT", g) for g in WD_KT[j]]
            src = w_dn.rearrange("(k p) c -> p k c", p=128)[:, :, oc * 128:(oc + 1) * 128]
            dma("pool", WD[j], src, [(r, "w") for r in regs], ("wd", j))
            return WD[j], regs

        def cload(dst, src, reg):
            dma("sp", dst, src, [(reg, "w")], "const")

        cload(ident[:], ident_d, "ident")
        cload(pmat[:], pmat_d, "pmat")
        cload(c2[:], c2_d.rearrange("p (k g) -> p k g", g=2), "c2")
        cload(bada[:], bada_d, "bada")
        cload(gvec[:], gvec_d.rearrange("p (a k) -> p a k", k=KC), "gvec")
        cload(qkg[:], qkg_d, "qkg")
        cload(convw[:], convw_d.rearrange("p (k j) -> p k j", j=3), "convw")
        cload(convf[:], convf_d.rearrange("p (k j) -> p k j", j=3), "convf")
        memset(ones_bf[:], 1.0, [("ones", "w")])
        memset(oblk_bf[:], 0.0, [("oblk", "w")])
        memset(oblk_bf[0:64, 0:64], 1.0, [("oblk", "w")])
        memset(oblk_bf[64:128, 64:128], 1.0, [("oblk", "w")])
        memset(V[:, :, :, 64:128], 1.0, [(("V", b_), "w") for b_ in range(20)])
        for i_ in range(2):
            memset(QT[i_][0][64:128, :], 0.0, [(("QT", i_), "w")])
            memset(QT[i_][1][0:64, :], 0.0, [(("QT", i_), "w")])
        memset(UB[:], 0.0, [("UB", "w")])
        memset(U2B[:], 0.0, [("U2B", "w")])
        for i_ in range(3):
            memset(XBV[i_][:], 0.0, [(("XBV", i_), "w")])

        def hslice(slot, k, c0=0, n=TT):
            return hT[:, k, slot * TT + c0: slot * TT + c0 + n]

        MO_ALL_W = [(("MO", k), "w") for k in range(KC)]
        MO_ALL_R = [(("MO", k), "r") for k in range(KC)]

        def load_xT(x_rows, slot):
            stg = MOs.rearrange("p (b d) -> p b d", d=D)
            dma("sp", stg, x_rows.rearrange("(b p) d -> p b d", p=128), MO_ALL_W, ("xs", 0))
            for k in range(KC):
                b = nextbank()
                for blk in range(4):
                    tr(bank(b)[:, blk * 128:(blk + 1) * 128], stg[:, blk, k * 128:(k + 1) * 128], ident[:],
                       MO_ALL_R + [("ident", "r"), (breg(b), "w")])
                act(hslice(slot, k), bank(b), AF.Copy, [(breg(b), "x"), (("hT", slot, k), "w")])

        P.stage = "load_xT"
        load_xT(x_p, 0)
        for i_ in range(1, 4):
            load_xT(x_s[i_ * TT:(i_ + 1) * TT, :], i_)
        P.stage = "ada"
        act(sc2[:], c2[:], AF.Sigmoid, [("c2", "r"), ("sc2", "w")])
        tt(sc2[:], sc2[:], c2[:], ALU.mult, [("c2", "r"), ("sc2", "r"), ("sc2", "w")])
        cp(sc2b[:], sc2[:], [("sc2", "r"), ("sc2b", "w")])
        MOb = MOs.bitcast(BF16)
        G24s = G24[:].rearrange("p c t -> p (c t)")
        for j in range(12):
            slot = j % 3
            m_ = slot
            stg = G24s[:, m_ * 4096:(m_ + 1) * 4096].rearrange("p (k c) -> p k c", c=512)
            sregs = [("G", 8 * m_ + q_) for q_ in range(8)]
            dma("pool", stg, wcols(w_ada, j * 512, 512), [(r, "w") for r in sregs], ("ada", slot))
            b = nextbank()
            for q_ in range(4):
                for k in range(KC):
                    mm(bank(b)[:, 2 * q_:2 * q_ + 2], stg[:, k, q_ * 128:(q_ + 1) * 128], sc2b[:, k, :], k == 0, k == KC - 1,
                       [(r, "r") for r in sregs] + [("sc2b", "r"), (breg(b), "w")])
            a_ = bada[:, 4 * j:4 * j + 1]
            bb_ = AP(a_.tensor, a_.offset, [list(a_.ap[0]), [1, 4], [0, 2]])
            tt(adaT[:, 4 * j:4 * j + 4, :], bank(b)[:, 0:8].rearrange("p (q e) -> p q e", e=2), bb_, ALU.add,
               [(breg(b), "x"), ("bada", "r"), ("adaT", "w")])
        for half in range(2):
            o = 24 * half
            gpre = gvec[:, 2 * half, :].unsqueeze(2).to_broadcast([128, KC, 2])
            gpost = gvec[:, 2 * half + 1, :].unsqueeze(2).to_broadcast([128, KC, 2])
            a_reg = [("adaT", "r"), ("gvec", "r"), ("MOD", "r"), ("MOD", "w")]
            ts(MOD[:, 3 * half + 0], adaT[:, o + 8:o + 16, :], 1.0, None, ALU.add, None, a_reg)
            tt(MOD[:, 3 * half + 0], MOD[:, 3 * half + 0], gpre, ALU.mult, a_reg)
            cp(MOD[:, 3 * half + 1], adaT[:, o:o + 8, :], a_reg)
            tt(MOD[:, 3 * half + 2], adaT[:, o + 16:o + 24, :], gpost, ALU.mult, a_reg)

        def modcol(which, k, grp):
            return MOD[:, which, k, grp:grp + 1]

        def rstd_gen(src_of_k, regs_of_k, nchunks, lhs_ones, ones_reg, inv_n, out_rs, out_reg, sbank, dve_sq=False):
            for k in range(nchunks):
                s_ = k % 2
                if dve_sq:
                    tt(SQ[s_][:], src_of_k(k), src_of_k(k), ALU.mult, regs_of_k(k) + [(("SQ", s_), "w")])
                else:
                    act(SQ[s_][:], src_of_k(k), AF.Square, regs_of_k(k) + [(("SQ", s_), "w")])
                yield
                mm(bank(sbank), lhs_ones, SQ[s_][:], k == 0, k == nchunks - 1,
                   [(("SQ", s_), "r"), (ones_reg, "r"), (breg(sbank), "w")])
            yield
            act(out_rs[:], bank(sbank), AF.Ln, [(breg(sbank), "x"), (out_reg, "w")], bias=EPS, scale=inv_n)
            act(out_rs[:], out_rs[:], AF.Exp, [(out_reg, "r"), (out_reg, "w")], scale=-0.5)
            yield

        def run(g):
            for _ in g:
                pass

        def interleave(main, side, ratio=1):
            n = 0
            for _ in main:
                n += 1
                if n % ratio == 0:
                    next(side, None)
            for _ in side:
                pass

        def chain(*gens):
            for g in gens:
                yield from g

        def pre_norm_gen(slot, which, grp, dve_sq=False):
            sbank = nextbank()
            reserved.add(sbank)
            yield from rstd_gen(lambda k: hslice(slot, k), lambda k: [(("hT", slot, k), "r")], KC, ones_bf[:], "ones",
                                1.0 / D, RS, "RS", sbank, dve_sq)
            reserved.discard(sbank)
            for k in range(KC):
                t = T[k % 2]
                tt(t[:], hslice(slot, k), RS[:], ALU.mult, [(("hT", slot, k), "r"), ("RS", "r"), (("T", k % 2), "w")])
                yield
                act(U[:, k, :], t[:], AF.Identity, [(("T", k % 2), "r"), ("MOD", "r"), (("U", k), "w")],
                    bias=modcol(3 * which + 1, k, grp), scale=modcol(3 * which + 0, k, grp))

        def pre_norm(slot, which, grp):
            run(pre_norm_gen(slot, which, grp))

        def post_norm_gen(slot, which, grp):
            sbank = nextbank()
            reserved.add(sbank)
            yield from rstd_gen(lambda k: MO[:, k, :], lambda k: [(("MO", k), "r")], KC, ones_bf[:], "ones",
                                1.0 / D, RS, "RS", sbank)
            reserved.discard(sbank)
            for k in range(KC):
                t = T[k % 2]
                stt(t[:], MO[:, k, :], modcol(3 * which + 2, k, grp), RS[:], ALU.mult, ALU.mult,
                    [(("MO", k), "r"), ("MOD", "r"), ("RS", "r"), (("T", k % 2), "w")])
                tt(hslice(slot, k), hslice(slot, k), t[:], ALU.add,
                   [(("hT", slot, k), "r"), (("T", k % 2), "r"), (("hT", slot, k), "w")])
                yield

        def post_norm_residual(slot, which, grp):
            run(post_norm_gen(slot, which, grp))

        def rstd_from(*args):
            run(rstd_gen(*args))

        TS_A = dict(sq=SQ[0][:], sqr=("SQ", 0), rs=T[2][:], rsr=("T", 2), kn=T[3][:], knr=("T", 3),
                    ra=T[0][:], rar=("T", 0), rb=T[1][:], rbr=("T", 1))
        TS_B = dict(sq=SQ[1][:], sqr=("SQ", 1), rs=MO[:, 4, :], rsr=("MO", 4), kn=MO[:, 5, :], knr=("MO", 5),
                    ra=MO[:, 6, :], rar=("MO", 6), rb=MO[:, 7, :], rbr=("MO", 7))

        def headnorm_gen(b, gcol, rope, out_ap, out_acc, sbank, pbank, ts=None):
            ts = ts or TS_A
            act(ts["sq"], bank(b), AF.Square, [(breg(b), "x"), (ts["sqr"], "w")])
            yield
            mm(bank(sbank), oblk_bf[:], ts["sq"], True, True, [(ts["sqr"], "r"), ("oblk", "r"), (breg(sbank), "w")])
            act(ts["rs"], bank(sbank), AF.Ln, [(breg(sbank), "x"), (ts["rsr"], "w")], bias=EPS, scale=1.0 / 64)
            act(ts["rs"], ts["rs"], AF.Exp, [(ts["rsr"], "r"), (ts["rsr"], "w")], scale=-0.5)
            stt(ts["kn"], bank(b), gcol, ts["rs"], ALU.mult, ALU.mult,
                [(breg(b), "x"), ("qkg", "r"), (ts["rsr"], "r"), (ts["knr"], "w")])
            yield
            outs = out_ap if isinstance(out_ap, list) else [(slice(0, 128), out_ap)]
            if rope:
                mm(bank(pbank), pmat[:], ts["kn"], True, True, [("pmat", "r"), (ts["knr"], "r"), (breg(pbank), "w")])
                tt(ts["ra"], ts["kn"], COS[:], ALU.mult, [(ts["knr"], "r"), ("COS", "r"), (ts["rar"], "w")])
                tt(ts["rb"], bank(pbank), SIN[:], ALU.mult, [(breg(pbank), "x"), ("SIN", "r"), (ts["rbr"], "w")])
                yield
                for rows, o_ap in outs:
                    tt(o_ap, ts["ra"][rows, :], ts["rb"][rows, :], ALU.add, [(ts["rar"], "r"), (ts["rbr"], "r")] + out_acc)
            else:
                for rows, o_ap in outs:
                    cp(o_ap, ts["kn"][rows, :], [(ts["knr"], "r")] + out_acc)

        def headnorm(*args):
            for _ in headnorm_gen(*args):
                pass

        def v_lhsT(blk, g):
            return V[:, blk, g, :]

        def load_rope(i):
            dma("sp", COS[:], cos_d[:, i * TT:(i + 1) * TT], [("COS", "w")], ("rope", 0))
            dma("sp", SIN[:], sin_d[:, i * TT:(i + 1) * TT], [("SIN", "w")], ("rope", 1))

        def roundrobin(*gens):
            gens = list(gens)
            while gens:
                for g_ in list(gens):
                    if next(g_, "done") == "done":
                        gens.remove(g_)

        def kv_stage(kpos, vblk0, rope, prompt):
            bn = []
            if prompt:
                bn = [nextbank(), nextbank()]
                reserved.update(bn)

            def kgen(g):
                ts = TS_A if g % 2 == 0 else TS_B
                src = wcols(w_in, C_K + g * 64, 64)
                wt, wr = load_w([(0, src), (64, src)])
                b = nextbank()
                reserved.add(b)
                for k in range(KC):
                    mm(bank(b), wt[:, k, :], U[:, k, :], k == 0, k == KC - 1,
                       [(wr, "r"), (("U", k), "r"), (breg(b), "w")])
                sbank = nextbank()
                reserved.add(sbank)
                yield from headnorm_gen(b, qkg[:, 1:2], rope, KTv[:, g, kpos:kpos + TT], [(("KT", g), "w")],
                                        sbank, sbank, ts)
                reserved.discard(b)
                reserved.discard(sbank)
                if prompt:
                    yield
                    for blk in range(4):
                        bb = bn[blk // 2]
                        c0 = (blk % 2) * 256 + g * 64
                        tr(bank(bb)[:, c0:c0 + 64], ts["kn"][0:64, blk * 128:(blk + 1) * 128], ident[0:64, 0:64],
                           [(ts["knr"], "r"), ("ident", "r"), (breg(bb), "w")])

            def vgen():
                wv = [load_w1(w_in, C_V), load_w1(w_in, C_V + 128)]
                for blk in range(4):
                    b = nextbank()
                    for hv in range(2):
                        wt, wr = wv[hv]
                        for k in range(KC):
                            mm(bank(b)[:, hv * 128:(hv + 1) * 128], U[:, k, blk * 128:(blk + 1) * 128], wt[:, k, :],
                               k == 0, k == KC - 1, [(wr, "r"), (("U", k), "r"), (breg(b), "w")])
                    act(V[:, vblk0 + blk, :, 0:64], bank(b)[:, 0:256].rearrange("p (g e) -> p g e", e=64), AF.Copy,
                        [(breg(b), "x"), (("V", vblk0 + blk), "w")])
                    if prompt:
                        c0 = 1024 + blk * 256
                        cp(MOs[:, c0:c0 + 256], bank(b)[:, 0:256], [(breg(b), "x"), (("MO", 2 + blk // 2), "w")])
                    yield

            roundrobin(kgen(0), kgen(1))
            roundrobin(kgen(2), kgen(3), vgen())
            if prompt:
                for i2 in range(2):
                    act(MO[:, i2, :], bank(bn[i2]), AF.Copy, [(breg(bn[i2]), "x"), (("MO", i2), "w")])
                reserved.difference_update(bn)
                dma("sp", nk_o.rearrange("(b p) c -> p b c", p=128),
                    MOs[:, 0:1024].rearrange("p (b c) -> p b c", c=256),
                    [(("MO", 0), "r"), (("MO", 1), "r")], ("st", 1), is_store=True)
                dma("sp", nv_o.rearrange("(b p) c -> p b c", p=128),
                    MOs[:, 1024:2048].rearrange("p (b c) -> p b c", c=256),
                    [(("MO", 2), "r"), (("MO", 3), "r")], ("st", 2), is_store=True)

        def q_proj_gen(c, rope):
            wt, wr = load_w1(w_in, C_Q + c * 128)
            for k in range(KC):
                mm(bank(0), wt[:, k, :], U[:, k, :], k == 0, k == KC - 1,
                   [(wr, "r"), (("U", k), "r"), (breg(0), "w")])
                if k == 3:
                    yield
            qs = QT[c % 2]
            yield from headnorm_gen(0, qkg[:, 0:1], rope,
                                    [(slice(0, 64), qs[0][0:64, :]), (slice(64, 128), qs[1][64:128, :])],
                                    [(("QT", c % 2), "w")], 1, 1)

        def attention(prompt, conv_i=None, pre_side=None):
            for _ in q_proj_gen(0, not prompt):
                pass
            for c in range(8):
                qgen = q_proj_gen(c + 1, not prompt) if c + 1 < 8 else iter(())
                qsteps = (0, 0, 1, 2, 3) if prompt else (2, 3, 7, 11, 14)
                cgen = conv_chunk_gen(conv_i, c) if (conv_i is not None and not prompt) else iter(())
                csteps = (8, 9, 12, 13, 14, 16, 17, 18)
                g = c // 2
                q = QT[c % 2]
                qreg = ("QT", c % 2)
                if prompt:
                    steps = [(hh, seq) for hh in range(2) for seq in range(2)]
                else:
                    steps = [(hh, j2) for hh in range(2) for j2 in range(10)]

                def s_step(n):
                    hh, j = steps[n]
                    rows = slice(hh * 64, hh * 64 + 64)
                    sd = 2 + (n % 2)
                    for half in range(2):
                        if prompt:
                            k0 = j * 256 + half * 128
                            mm(PS[sd][:, half, 0:256], KTv[:, g, k0:k0 + 128], q[hh][:, j * 256:(j + 1) * 256],
                               True, True, [(("KT", g), "r"), (qreg, "r"), (breg(2 * sd + half), "w")])
                        else:
                            k0 = (2 * j + half) * 128
                            mm(PS[sd][:, half, :], KTv[:, g, k0:k0 + 128], q[hh][:, :],
                               True, True, [(("KT", g), "r"), (qreg, "r"), (breg(2 * sd + half), "w")])

                def e_step(n):
                    sd = 2 + (n % 2)
                    pt = PT[n % 3]
                    acc = [(breg(2 * sd), "x"), (breg(2 * sd + 1), "x"), (("PT", n % 3), "w")]
                    if prompt:
                        act(pt[:, :, 0:256], PS[sd][:, :, 0:256], AF.Exp, acc, scale=0.125)
                    else:
                        act(pt[:], PS[sd][:], AF.Exp, acc, scale=0.125)

                def pv_step(n):
                    hh, j = steps[n]
                    ob = 2 + hh
                    pt = PT[n % 3]
                    for half in range(2):
                        if prompt:
                            blk = j * 2 + half
                            mm(bank(ob)[:, j * 256:(j + 1) * 256], v_lhsT(blk, g), pt[:, half, 0:256],
                               half == 0, half == 1,
                               [(("V", blk), "r"), (("PT", n % 3), "r"), (breg(ob), "w")])
                        else:
                            blk = 2 * j + half
                            mm(bank(ob), v_lhsT(blk, g), pt[:, half, :], blk == 0, blk == 19,
                               [(("V", blk), "r"), (("PT", n % 3), "r"), (breg(ob), "w")])

                def fin_gen(hh):
                    ob = 2 + hh
                    yield
                    act(RS[0:64, :], bank(ob)[64:128, :], AF.Ln, [(breg(ob), "x"), ("RS", "w")])
                    yield
                    act(RS[0:64, :], RS[0:64, :], AF.Exp, [("RS", "r"), ("RS", "w")], scale=-1.0)
                    tt(ATT(c)[hh * 64:(hh + 1) * 64, :], bank(ob)[0:64, :], RS[0:64, :], ALU.mult,
                       [(breg(ob), "x"), ("RS", "r"), (("G", c), "w")])

                fins = []

                ns = len(steps)

                def pv_and_fin(m):
                    pv_step(m)
                    if m + 1 == ns or steps[m + 1][0] != steps[m][0]:
                        fins.append(fin_gen(steps[m][0]))

                s_step(0)
                if ns > 1:
                    s_step(1)
                for n in range(ns):
                    e_step(n)
                    for fg in list(fins):
                        if next(fg, "done") == "done":
                            fins.remove(fg)
                    if n >= 1:
                        pv_and_fin(n - 1)
                    for _q in range(qsteps.count(n)):
                        next(qgen, None)
                    if conv_i is not None and n in csteps:
                        next(cgen, None)
                    if pre_side is not None and c == 0:
                        next(pre_side, None)
                    if n + 2 < ns:
                        s_step(n + 2)
                pv_and_fin(ns - 1)
                for fg in fins:
                    for _ in fg:
                        pass
                for _ in qgen:
                    pass
                for _ in cgen:
                    pass
                if pre_side is not None and c == 0:
                    for _ in pre_side:
                        pass
                if prompt and conv_i is not None:
                    for _ in conv_chunk_gen(conv_i, c, prompt=True):
                        pass

        def tile_views(prompt):
            if prompt:
                def v512(ap):
                    return ap.rearrange("p (s n) -> p s n", s=2)

                def xviews(buf):
                    b3 = buf[:, 0:516].rearrange("p (s n) -> p s n", s=2)
                    return b3[:, :, 1:257], b3[:, :, 0:256], b3[:, :, 2:258]
            else:
                def v512(ap):
                    return ap

                def xviews(buf):
                    return buf[:, 1:513], buf[:, 0:512], buf[:, 2:514]
            return v512, xviews

        def halo_cols(buf, k, i):
            a = buf[:, k, 2 * i:2 * i + 1]
            return AP(a.tensor, a.offset, [list(a.ap[0]), [3, 2]])

        def edge_cols(buf):
            a = buf[:, 0:1]
            return AP(a.tensor, a.offset, [list(a.ap[0]), [513, 2]])

        def conv_mixer(prompt, i):
            v512, xviews = tile_views(prompt)
            xc, xl, xr = xviews(XB)
            for c in range(8):
                wb, wbr = load_w1(w_in, C_B + c * 128)
                wc, wcr = load_w1(w_in, C_C + c * 128)
                wx, wxr = load_w1(w_in, C_X + c * 128)
                bb, bc, bx = nextbank(), nextbank(), nextbank()
                for (wt, wr, b) in ((wb, wbr, bb), (wc, wcr, bc), (wx, wxr, bx)):
                    for k in range(KC):
                        mm(bank(b), wt[:, k, :], U[:, k, :], k == 0, k == KC - 1,
                           [(wr, "r"), (("U", k), "r"), (breg(b), "w")])
                if not prompt:
                    bh = nextbank()
                    for n2, (wt, wr) in enumerate(((wc, wcr), (wx, wxr))):
                        for k in range(KC):
                            mm(bank(bh)[:, 2 * n2:2 * n2 + 2], wt[:, k, :], halo_cols(UB, k, i), k == 0, k == KC - 1,
                               [(wr, "r"), ("UB", "r"), (breg(bh), "w")])
                act(T[0][:], bank(bx), AF.Copy, [(breg(bx), "x"), (("T", 0), "w")])
                tt(xc, v512(bank(bc)), v512(T[0][:]), ALU.mult, [(breg(bc), "x"), (("T", 0), "r"), (("XBV", 0), "w")])
                if not prompt:
                    act(T[1][:, 0:2], bank(bh)[:, 2:4], AF.Copy, [(breg(bh), "x"), (("T", 1), "w")])
                    tt(edge_cols(XB), bank(bh)[:, 0:2], T[1][:, 0:2], ALU.mult,
                       [(breg(bh), "x"), (("T", 1), "r"), (("XBV", 0), "w")])
                y = v512(T[2][:])
                yreg = ("T", 2)
                ts(y, xl, convw[:, c, 0:1], None, ALU.mult, None, [(("XBV", 0), "r"), ("convw", "r"), (yreg, "w")])
                stt(y, xc, convw[:, c, 1:2], y, ALU.mult, ALU.add, [(("XBV", 0), "r"), ("convw", "r"), (yreg, "r"), (yreg, "w")])
                stt(y, xr, convw[:, c, 2:3], y, ALU.mult, ALU.add, [(("XBV", 0), "r"), ("convw", "r"), (yreg, "r"), (yreg, "w")])
                tt(CONV(c), T[2][:], bank(bb), ALU.mult, [(yreg, "r"), (breg(bb), "x"), (("G", 8 + c), "w")])

        def conv_chunk_gen(i, c, prompt=False):
            v512, xviews = tile_views(prompt)
            xc, xl, xr = xviews(XB)
            xreg = ("XBV", 0)
            ct0, ct1, ct2 = MO[:, 0, :], MO[:, 1, :], MO[:, 2, :]
            wc, wcr = load_w1(w_in, C_C + c * 128)
            wx, wxr = load_w1(w_in, C_X + c * 128)
            wb, wbr = load_w1(w_in, C_B + c * 128)
            for k in range(KC):
                mm(bank(0), wc[:, k, :], U[:, k, :], k == 0, k == KC - 1, [(wcr, "r"), (("U", k), "r"), (breg(0), "w")])
                if k == 3:
                    yield
            yield
            for k in range(KC):
                mm(bank(1), wx[:, k, :], U[:, k, :], k == 0, k == KC - 1, [(wxr, "r"), (("U", k), "r"), (breg(1), "w")])
                if k == 3:
                    yield
            yield
            cp(ct0, bank(1), [(breg(1), "x"), (("MO", 0), "w")])
            tt(xc, v512(bank(0)), v512(ct0), ALU.mult, [(breg(0), "x"), (("MO", 0), "r"), (xreg, "w")])
            yield
            if not prompt:
                for n2, (wt, wr) in enumerate(((wc, wcr), (wx, wxr))):
                    for k in range(KC):
                        mm(bank(0)[:, 2 * n2:2 * n2 + 2], wt[:, k, :], halo_cols(UB, k, i), k == 0, k == KC - 1,
                           [(wr, "r"), ("UB", "r"), (breg(0), "w")])
            for k in range(KC):
                mm(bank(1), wb[:, k, :], U[:, k, :], k == 0, k == KC - 1, [(wbr, "r"), (("U", k), "r"), (breg(1), "w")])
                if k == 3:
                    yield
            yield
            if not prompt:
                cp(ct1[:, 0:2], bank(0)[:, 2:4], [(breg(0), "x"), (("MO", 1), "w")])
                tt(edge_cols(XB), bank(0)[:, 0:2], ct1[:, 0:2], ALU.mult,
                   [(breg(0), "x"), (("MO", 1), "r"), (xreg, "w")])
            y = v512(ct2)
            yacc = [(xreg, "r"), ("convw", "r"), (("MO", 2), "r"), (("MO", 2), "w")]
            ts(y, xl, convw[:, c, 0:1], None, ALU.mult, None, [(xreg, "r"), ("convw", "r"), (("MO", 2), "w")])
            stt(y, xc, convw[:, c, 1:2], y, ALU.mult, ALU.add, yacc)
            stt(y, xr, convw[:, c, 2:3], y, ALU.mult, ALU.add, yacc)
            tt(CONV(c), ct2, bank(1), ALU.mult, [(("MO", 2), "r"), (breg(1), "x"), (("G", 8 + c), "w")])

        def merge():
            for oc in range(8):
                wa, war = load_w1(w_att, oc * 128)
                wcv, wcvr = load_w1(w_cvo, oc * 128)
                wga, wgar = load_w1(w_in, C_GA + oc * 128)
                wgc, wgcr = load_w1(w_in, C_GC + oc * 128)
                ba, bcv, bga, bgc = nextbank(), nextbank(), nextbank(), nextbank()
                for k in range(KC):
                    mm(bank(ba), wa[:, k, :], ATT(k), k == 0, k == KC - 1, [(war, "r"), (("G", k), "r"), (breg(ba), "w")])
                for k in range(KC):
                    mm(bank(bcv), wcv[:, k, :], CONV(k), k == 0, k == KC - 1,
                       [(wcvr, "r"), (("G", 8 + k), "r"), (breg(bcv), "w")])
                for (wt, wr, b) in ((wga, wgar, bga), (wgc, wgcr, bgc)):
                    for k in range(KC):
                        mm(bank(b), wt[:, k, :], U[:, k, :], k == 0, k == KC - 1,
                           [(wr, "r"), (("U", k), "r"), (breg(b), "w")])
                act(T[0][:], bank(bga), AF.Sigmoid, [(breg(bga), "x"), (("T", 0), "w")])
                act(T[1][:], bank(bgc), AF.Sigmoid, [(breg(bgc), "x"), (("T", 1), "w")])
                tt(T[2][:], bank(ba), T[0][:], ALU.mult, [(breg(ba), "x"), (("T", 0), "r"), (("T", 2), "w")])
                tt(T[3][:], bank(bcv), T[1][:], ALU.mult, [(breg(bcv), "x"), (("T", 1), "r"), (("T", 3), "w")])
                tt(MERGED(oc), T[2][:], T[3][:], ALU.add, [(("T", 2), "r"), (("T", 3), "r"), (("G", 16 + oc), "w")])

        RS2 = XBV[1][:, 0, 0:TT]
        TP2 = XBV[1][:, 1, 0:TT]
        SQ2 = [XBV[2][:, 0, :].bitcast(BF16)[:, 0:TT], XBV[2][:, 0, :].bitcast(BF16)[:, TT:2 * TT]]

        def w_o_proj_gen(fold_stats=False):
            sb2 = None
            if fold_stats:
                sb2 = nextbank()
                reserved.add(sb2)

            def stat_mm(oc):
                mm(bank(sb2), ones_bf[:], SQ2[oc % 2], oc == 0, oc == 7,
                   [(("XBV", 2), "r"), ("ones", "r"), (breg(sb2), "w")])

            for oc in range(8):
                wt, wr = load_w1(w_o, oc * 128)
                b = nextbank()
                for k in range(KC):
                    mm(bank(b), wt[:, k, :], MERGED(k), k == 0, k == KC - 1,
                       [(wr, "r"), (("G", 16 + k), "r"), (breg(b), "w")])
                    if k == 3:
                        reserved.add(b)
                        yield
                        reserved.discard(b)
                if fold_stats and oc >= 1:
                    stat_mm(oc - 1)
                act(MO[:, oc, :], bank(b), AF.Copy, [(breg(b), "x"), (("MO", oc), "w")])
                if fold_stats:
                    act(SQ2[oc % 2], MO[:, oc, :], AF.Square, [(("MO", oc), "r"), (("XBV", 2), "w")])
                yield
            if fold_stats:
                stat_mm(7)
                act(RS2, bank(sb2), AF.Ln, [(breg(sb2), "x"), (("XBV", 1), "w")], bias=EPS, scale=1.0 / D)
                act(RS2, RS2, AF.Exp, [(("XBV", 1), "r"), (("XBV", 1), "w")], scale=-0.5)
                reserved.discard(sb2)

        def post_apply_gen(slot, which, grp):
            for k in range(KC):
                stt(TP2, MO[:, k, :], modcol(3 * which + 2, k, grp), RS2, ALU.mult, ALU.mult,
                    [(("MO", k), "r"), ("MOD", "r"), (("XBV", 1), "r"), (("XBV", 1), "w")])
                tt(hslice(slot, k), hslice(slot, k), TP2, ALU.add,
                   [(("hT", slot, k), "r"), (("XBV", 1), "r"), (("hT", slot, k), "w")])
                yield

        def w_o_proj():
            run(w_o_proj_gen())

        def nextpair():
            while True:
                b = bank_rr[0]
                if b % 2:
                    b = (b + 1) % 8
                bank_rr[0] = (b + 2) % 8
                if b not in reserved and b + 1 not in reserved:
                    return b // 2

        def ffn_up_gen(prompt, i):
            v512, xviews = tile_views(prompt)

            def bufs(f):
                st = f % 3
                return (XBV[st], ("XBV", st), FT[3 * st], FT[3 * st + 1], FT[3 * st + 2],
                        FTR[3 * st], FTR[3 * st + 1], FTR[3 * st + 2])

            def front(f):
                xbv, xreg, tg, tv, tsl, rg, rv, rsl = bufs(f)
                gc, gl, gr = xviews(xbv[:, 0, :])
                vc, vl, vr = xviews(xbv[:, 1, :])
                wg, wgr = load_w1(w_up, f * 128)
                wv, wvr = load_w1(w_up, DFF + f * 128)
                d = nextpair()
                bg, bv = 2 * d, 2 * d + 1
                for (wt, wr, b) in ((wg, wgr, bg), (wv, wvr, bv)):
                    for k in range(KC):
                        mm(bank(b), wt[:, k, :], U[:, k, :], k == 0, k == KC - 1,
                           [(wr, "r"), (("U", k), "r"), (breg(b), "w")])
                if not prompt:
                    bh = nextbank()
                    for n2, (wt, wr) in enumerate(((wg, wgr), (wv, wvr))):
                        for k in range(KC):
                            mm(bank(bh)[:, 2 * n2:2 * n2 + 2], wt[:, k, :], halo_cols(U2B, k, i), k == 0, k == KC - 1,
                               [(wr, "r"), ("U2B", "r"), (breg(bh), "w")])
                act(v512(tg), v512(bank(bg)), AF.Identity, [(breg(bg), "x"), ("convf", "r")] + [(r, "w") for r in rg],
                    scale=convf[:, f, 1:2])
                act(v512(tv), v512(bank(bv)), AF.Identity, [(breg(bv), "x"), ("convf", "r")] + [(r, "w") for r in rv],
                    scale=convf[:, NFC + f, 1:2])
                if prompt:
                    act(gc, v512(bank(bg)), AF.Copy, [(breg(bg), "x"), (xreg, "w")])
                    act(vc, v512(bank(bv)), AF.Copy, [(breg(bv), "x"), (xreg, "w")])
                else:
                    act(xbv[:, :, 1:513], PS[d][:], AF.Copy, [(breg(bg), "x"), (breg(bv), "x"), (xreg, "w")])
                    act(edge_cols(xbv[:, 0, :]), bank(bh)[:, 0:2], AF.Copy, [(breg(bh), "x"), (xreg, "w")])
                    act(edge_cols(xbv[:, 1, :]), bank(bh)[:, 2:4], AF.Copy, [(breg(bh), "x"), (xreg, "w")])

            def taps(f):
                xbv, xreg, tg, tv, tsl, rg, rv, rsl = bufs(f)
                gc, gl, gr = xviews(xbv[:, 0, :])
                vc, vl, vr = xviews(xbv[:, 1, :])
                for (l, r, tcol, tbuf, treg) in ((gl, gr, f, tg, rg), (vl, vr, NFC + f, tv, rv)):
                    y = v512(tbuf)
                    tacc = [(xreg, "r"), ("convf", "r")] + [(r_, "r") for r_ in treg] + [(r_, "w") for r_ in treg]
                    stt(y, l, convf[:, tcol, 0:1], y, ALU.mult, ALU.add, tacc)
                    stt(y, r, convf[:, tcol, 2:3], y, ALU.mult, ALU.add, tacc)

            def silu(f):
                xbv, xreg, tg, tv, tsl, rg, rv, rsl = bufs(f)
                act(tsl, tg, AF.Silu, [(r, "r") for r in rg] + [(r, "w") for r in rsl])

            def fin(f):
                xbv, xreg, tg, tv, tsl, rg, rv, rsl = bufs(f)
                tt(ACTF(f), tsl, tv, ALU.mult, [(r, "r") for r in rsl] + [(r, "r") for r in rv] + [(("G", f), "w")])

            for f in range(NFC):
                front(f)
                if f >= 1:
                    silu(f - 1)
                taps(f)
                if f >= 1:
                    fin(f - 1)
                yield
            silu(NFC - 1)
            fin(NFC - 1)

        def ffn_down_gen():
            wdv = w_dn.rearrange("(k p) c -> p k c", p=128)
            for half in range(2):
                bks = [nextbank() for _ in range(4)]
                reserved.update(bks)
                for kf2 in range(NFC // 2):
                    wt, wr = load_w([(0, wdv[:, 2 * kf2:2 * kf2 + 2, half * 512:(half + 1) * 512])])
                    wflat = wt[:].rearrange("p k c -> p (k c)")
                    for kk in range(2):
                        kf = 2 * kf2 + kk
                        for q_ in range(4):
                            mm(bank(bks[q_]), wflat[:, kk * 512 + q_ * 128:kk * 512 + (q_ + 1) * 128], ACTF(kf),
                               kf == 0, kf == NFC - 1, [(wr, "r"), (("G", kf), "r"), (breg(bks[q_]), "w")])
                    yield
                reserved.difference_update(bks)
                for q_ in range(4):
                    oc = half * 4 + q_
                    act(MO[:, oc, :], bank(bks[q_]), AF.Copy, [(breg(bks[q_]), "x"), (("MO", oc), "w")])
                yield

        def store_out_gen(slot, y_rows):
            stg = MOs.rearrange("p (b d) -> p b d", d=D)
            for blk in range(4):
                for half in range(2):
                    b = nextbank()
                    for kk in range(4):
                        k = half * 4 + kk
                        tr(bank(b)[:, kk * 128:(kk + 1) * 128], hslice(slot, k, blk * 128, 128), ident[:],
                           [(("hT", slot, k), "r"), ("ident", "r"), (breg(b), "w")])
                    reserved.add(b)
                    yield
                    reserved.discard(b)
                    act(stg[:, blk, half * 512:(half + 1) * 512], bank(b), AF.Copy,
                        [(breg(b), "x"), (("MO", 2 * blk + half), "w")])
                dma("sp", y_rows[blk * 128:(blk + 1) * 128, :], stg[:, blk, :],
                    [(("MO", 2 * blk), "r"), (("MO", 2 * blk + 1), "r")], ("st", 10 + blk), is_store=True)
            yield

        def store_out(slot, y_rows):
            run(store_out_gen(slot, y_rows))

        def edge_prenorm(slot, i, grp):
            a0 = hT[:, 0, slot * TT:slot * TT + 1]
            cols = AP(a0.tensor, a0.offset, [list(a0.ap[0]), [NS, KC], [TT - 1, 2]])
            hregs = [(("hT", slot, k), "r") for k in range(KC)]
            act(SQe[:], cols, AF.Square, hregs + [("SQe", "w")])
            b = nextbank()
            for k in range(KC):
                mm(bank(b)[:, 0:2], ones_bf[:], SQe[:, k, :], k == 0, k == KC - 1,
                   [("SQe", "r"), ("ones", "r"), (breg(b), "w")])
            act(RSe[:], bank(b)[:, 0:2], AF.Ln, [(breg(b), "x"), ("RSe", "w")], bias=EPS, scale=1.0 / D)
            act(RSe[:], RSe[:], AF.Exp, [("RSe", "r"), ("RSe", "w")], scale=-0.5)
            r0 = RSe[:, 0:1]
            rsb = AP(r0.tensor, r0.offset, [list(r0.ap[0]), [0, KC], [1, 2]])
            m0 = MOD[:, 3, 0, grp:grp + 1]
            a2b = AP(m0.tensor, m0.offset, [list(m0.ap[0]), [2, KC], [0, 2]])
            m1 = MOD[:, 4, 0, grp:grp + 1]
            b2b = AP(m1.tensor, m1.offset, [list(m1.ap[0]), [2, KC], [0, 2]])
            tt(Te[:], cols, rsb, ALU.mult, hregs + [("RSe", "r"), ("Te", "w")])
            tt(Te[:], Te[:], a2b, ALU.mult, [("Te", "r"), ("MOD", "r"), ("Te", "w")])
            tt(U2B[:, :, 1 + 2 * i:3 + 2 * i], Te[:], b2b, ALU.add, [("Te", "r"), ("MOD", "r"), ("U2B", "w")])

        def save_edges(dst, i):
            a = U[:, :, 0:1]
            src = AP(a.tensor, a.offset, [list(a.ap[0]), list(a.ap[1]), [511, 2]])
            cp(dst[:, :, 1 + 2 * i:3 + 2 * i], src, [(("U", k), "r") for k in range(KC)] + [("UB" if dst is UB else "U2B", "w")])

        P.stage = "pre_norm"; pre_norm(0, 0, 1)
        P.stage = "kv_stage"; kv_stage(0, 0, False, True)
        P.stage = "attention"; attention(True, conv_i=0)
        P.stage = "merge"; merge()
        P.stage = "w_o_proj"; w_o_proj()
        P.stage = "post_norm_residual"; post_norm_residual(0, 0, 1)
        P.stage = "pre_norm"; pre_norm(0, 1, 1)
        P.stage = "ffn"; run(ffn_up_gen(True, 0)); run(ffn_down_gen())
        P.stage = "post_norm_residual"; post_norm_residual(0, 1, 1)
        P.stage = "store_out"; store_out(0, y_p)
        memset(V[:, :, :, 64:128], 1.0, [(("V", b_), "w") for b_ in range(20)])

        P.stage = "cache"
        ckbuf = [XBV[1 + h_][:].rearrange("p h n -> p (h n)")[:, 0:1024].rearrange("p (b g u e) -> p b g u e", b=2, g=4, u=2)
                 for h_ in range(2)]
        ckv = ck.rearrange("(b p) (g e) -> p b g e", p=128, e=64)
        for u in range(2):
            for blk in range(4):
                dma("sp", ckbuf[blk // 2][:, blk % 2, :, u, :], ckv[:, blk, :, :], [(("XBV", 1 + blk // 2), "w")], "cache")
        cvv = cv.rearrange("(b p) (g e) -> p b g e", p=128, e=64)
        for blk in range(4):
            dma("pool", V[:, blk, :, 0:64], cvv[:, blk, :, :], [(("V", blk), "w")], "cachev")
        for g in range(4):
            b = nextbank()
            for blk in range(4):
                tr(bank(b)[:, blk * 128:(blk + 1) * 128],
                   ckbuf[blk // 2][:, blk % 2, g, :, :].rearrange("p u e -> p (u e)"), ident[:],
                   [(("XBV", 1), "r"), (("XBV", 2), "r"), ("ident", "r"), (breg(b), "w")])
            act(KTv[:, g, 0:PAST], bank(b), AF.Copy, [(breg(b), "x"), (("KT", g), "w")])
        for i in range(4):
            if i == 0:
                P.stage = "load_xT"; load_xT(x_s[0:TT, :], 0)
            P.stage = "pre_norm"; pre_norm(i, 0, 0)
            P.stage = "save_edges"; save_edges(UB, i)
            P.stage = "load_rope"; load_rope(i)
            P.stage = "kv_stage"; kv_stage(PAST + i * TT, 4 + 4 * i, True, False)
        P.stage = "pre_norm"; pre_norm(0, 0, 0)

        for i in range(4):
            P.stage = "load_rope"; load_rope(i)
            P.stage = "attention"; attention(False, conv_i=i, pre_side=post_apply_gen(i - 1, 0, 0) if i > 0 else None)
            if i > 0:
                P.stage = "save_edges"; edge_prenorm(i - 1, i - 1, 0)
            P.stage = "merge"; merge()
            P.stage = "w_o_proj"; interleave(w_o_proj_gen(fold_stats=True),
                                              pre_norm_gen(i + 1, 0, 0, dve_sq=True) if i < 3
                                              else pre_norm_gen(0, 1, 0, dve_sq=True), 1)
        P.stage = "post_norm_residual"; run(post_apply_gen(3, 0, 0))
        P.stage = "save_edges"; edge_prenorm(3, 3, 0)
        for i in range(4):
            P.stage = "ffn"
            side = post_norm_gen(i - 1, 1, 0) if i > 0 else iter(())
            interleave(ffn_up_gen(False, i), side, 1)
            side2 = chain(store_out_gen(i - 1, y_s[(i - 1) * TT:i * TT, :]) if i > 0 else iter(()),
                          pre_norm_gen(i + 1, 1, 0) if i < 3 else iter(()))
            interleave(ffn_down_gen(), side2, 1)
        P.stage = "post_norm_residual"; post_norm_residual(3, 1, 0)
        P.stage = "store_out"; store_out(3, y_s[3 * TT:4 * TT, :])

        P.emit(nc, es)
    build_program.last_prog = P
    return nc


def _rope_consts():
    t = np.arange(NS)
    row = (t // GRID_W).astype(np.float32)
    col = (t % GRID_W).astype(np.float32)
    inv = np.power(np.float32(10000.0), -np.arange(0, 32, 2, dtype=np.float32) / np.float32(32)).astype(np.float32)
    cos = np.zeros((128, NS), np.float32)
    sin = np.zeros((128, NS), np.float32)
    for p in range(128):
        d = p % 64
        pos = row if d < 32 else col
        ang = (pos * inv[d % 16]).astype(np.float32)
        cos[p] = np.cos(ang)
        sin[p] = np.sin(ang)
    pm = np.zeros((128, 128), np.float32)
    for m in range(128):
        j = m % 32
        if j < 16:
            pm[m + 16, m] = -1.0
        else:
            pm[m - 16, m] = 1.0
    return cos, sin, pm


def _col_layout(v):
    return np.ascontiguousarray(np.asarray(v, np.float32).reshape(-1, 128).T)


_NC_CACHE = {}


def kernel(x_prompt, x_sample, cache_k, cache_v, c, c_ctx, w_ada, b_ada, g_pre1, g_post1, g_pre2, g_post2,
           w_in, q_norm, k_norm, w_att_out, conv_w, w_conv_out, w_o, w_up, conv_ffn, w_down):
    f = lambda a: np.ascontiguousarray(np.asarray(a, dtype=np.float32))
    x_prompt, x_sample, cache_k, cache_v = f(x_prompt), f(x_sample), f(cache_k), f(cache_v)
    n = 8
    cos, sin, pm = _rope_consts()
    shared = {
        "bada": _col_layout(f(b_ada)[0]),
        "gvec": np.ascontiguousarray(np.stack([_col_layout(f(g)[0]) for g in (g_pre1, g_post1, g_pre2, g_post2)],
                                              axis=1).reshape(128, 32)),
        "qkg": np.ascontiguousarray(np.stack([np.tile(f(q_norm)[0], 2), np.tile(f(k_norm)[0], 2)], axis=1)),
        "convw": np.ascontiguousarray(np.stack([_col_layout(f(conv_w)[0, j]) for j in range(3)], axis=2).reshape(128, 24)),
        "convf": np.ascontiguousarray(np.stack([_col_layout(f(conv_ffn)[0, j]) for j in range(3)], axis=2).reshape(128, 132)),
        "ident": np.eye(128, dtype=np.float32),
        "pmat": pm, "cos": cos, "sin": sin,
        "w_ada": f(w_ada)[0], "w_in": f(w_in)[0], "w_att": f(w_att_out)[0], "w_cvo": f(w_conv_out)[0],
        "w_o": f(w_o)[0], "w_up": f(w_up)[0], "w_dn": f(w_down)[0],
    }
    cc = _col_layout(f(c_ctx))
    in_maps = []
    for i in range(n):
        m = dict(shared)
        m["x_s"] = x_sample[i]
        m["x_p"] = x_prompt[2 * i:2 * i + 2].reshape(NP, D)
        m["ck"] = cache_k[i, 0].reshape(PAST, 256)
        m["cv"] = cache_v[i, 0].reshape(PAST, 256)
        m["c2"] = np.ascontiguousarray(np.stack([_col_layout(f(c)[i]), cc], axis=2).reshape(128, 16))
        in_maps.append(m)
    if "nc" not in _NC_CACHE:
        _NC_CACHE["nc"] = build_program()
    res = run_bass_kernel_spmd(_NC_CACHE["nc"], in_maps, core_ids=list(range(n)))
    y_prompt = np.empty((16, 256, D), np.float32)
    y_sample = np.empty((8, NS, D), np.float32)
    new_k = np.empty((16, 1, 256, 4, 64), np.float32)
    new_v = np.empty((16, 1, 256, 4, 64), np.float32)
    for i, r in enumerate(res.results):
        y_prompt[2 * i:2 * i + 2] = np.asarray(r["y_p"]).reshape(2, 256, D)
        y_sample[i] = np.asarray(r["y_s"])
        new_k[2 * i:2 * i + 2, 0] = np.asarray(r["nk"]).reshape(2, 256, 4, 64)
        new_v[2 * i:2 * i + 2, 0] = np.asarray(r["nv"]).reshape(2, 256, 4, 64)
    return (y_prompt, y_sample, new_k, new_v)
```
